# Optimizing a Trainium2 kernel written in Bass

```python
import math
import jax, jax.numpy as jnp
from jax import lax
import numpy as np

D_MODEL = 4096
BATCH = 2
SEQ = 4096
DEPTH = 2

CHUNK = 64
HEAD_DIM = 128
N_HEADS = D_MODEL // HEAD_DIM
MIX = N_HEADS * HEAD_DIM
GDN_CONV = 4
FFN_CONV = 3
FFN_DIM = ((8 * D_MODEL // 3 + 255) // 256) * 256
Q_BLOCK = 128
N_MIXERS = 2
N_GDN = (DEPTH + 1) // 2
N_FOX = DEPTH // 2
GDN_IN = 4 * MIX + 2 * N_HEADS
FOX_IN = 4 * MIX + N_HEADS
EPS = 1e-6

kernel_name = 'chunk_causal_gdn_fox_hybrid'


def rms_norm(x, g):
    xf = x.astype(jnp.float32)
    y = xf * lax.rsqrt(jnp.mean(xf * xf, axis=-1, keepdims=True) + EPS)
    return (y * g.astype(jnp.float32)).astype(x.dtype)


def l2_norm(x):
    return x * lax.rsqrt(jnp.sum(x * x, axis=-1, keepdims=True) + EPS)


def causal_dwconv(x, w):
    k_w = w.shape[0]
    s = x.shape[1]
    xp = jnp.pad(x, ((0, 0), (k_w - 1, 0), (0, 0)))
    return sum(xp[:, j:j + s] * w[j] for j in range(k_w))


def gated_delta_rule(q, k, v, beta, log_a):
    b, h, s, dk = q.shape
    dv = v.shape[-1]
    n = s // CHUNK
    q = q.reshape(b, h, n, CHUNK, dk)
    k = k.reshape(b, h, n, CHUNK, dk)
    v = v.reshape(b, h, n, CHUNK, dv)
    beta = beta.reshape(b, h, n, CHUNK)
    g = jnp.cumsum(log_a.reshape(b, h, n, CHUNK), axis=-1)
    incl = jnp.tril(jnp.ones((CHUNK, CHUNK), dtype=bool))
    strict = jnp.tril(jnp.ones((CHUNK, CHUNK), dtype=bool), -1)
    decay = jnp.exp(jnp.where(incl, g[..., :, None] - g[..., None, :], -jnp.inf))
    kb = k * beta[..., None]
    vb = v * beta[..., None]
    m = jnp.where(strict, jnp.einsum('bhncd,bhnsd->bhncs', kb, k) * decay, 0.0)
    eye = jnp.eye(CHUNK, dtype=jnp.float32)
    t_mat = lax.linalg.triangular_solve(eye + m, jnp.broadcast_to(eye, m.shape),
                                        left_side=True, lower=True, unit_diagonal=True)
    u = jnp.einsum('bhncs,bhnsv->bhncv', t_mat, vb)
    w = jnp.einsum('bhncs,bhnsd->bhncd', t_mat, kb * jnp.exp(g)[..., None])
    attn = jnp.where(incl, jnp.einsum('bhncd,bhnsd->bhncs', q, k) * decay, 0.0)
    g_last = g[..., -1]
    q_head = q * jnp.exp(g)[..., None]
    k_tail = k * jnp.exp(g_last[..., None] - g)[..., None]

    def step(state, inp):
        u_i, w_i, qh_i, kt_i, a_i, gl_i = inp
        v_new = u_i - jnp.einsum('bhck,bhkv->bhcv', w_i, state)
        o_i = jnp.einsum('bhck,bhkv->bhcv', qh_i, state) + jnp.einsum('bhcs,bhsv->bhcv', a_i, v_new)
        state = state * jnp.exp(gl_i)[..., None, None] + jnp.einsum('bhck,bhcv->bhkv', kt_i, v_new)
        return state, o_i

    front = lambda t: jnp.moveaxis(t, 2, 0)
    state0 = jnp.zeros((b, h, dk, dv), jnp.float32)
    _, o = lax.scan(step, state0, (front(u), front(w), front(q_head), front(k_tail), front(attn), front(g_last)))
    return jnp.moveaxis(o, 0, 2).reshape(b, h, s, dv)


def gdn_mixer(hx, w_in, conv_w, a_log_param, dt_bias, norm_g, w_out):
    b, s, _ = hx.shape
    proj = hx @ w_in
    qkv = jax.nn.silu(causal_dwconv(proj[..., :3 * MIX], conv_w))
    z = proj[..., 3 * MIX:4 * MIX]
    b_raw = proj[..., 4 * MIX:4 * MIX + N_HEADS].astype(jnp.float32)
    a_raw = proj[..., 4 * MIX + N_HEADS:].astype(jnp.float32)
    heads = lambda t: t.astype(jnp.float32).reshape(b, s, N_HEADS, HEAD_DIM).transpose(0, 2, 1, 3)
    q = l2_norm(heads(qkv[..., :MIX])) * (HEAD_DIM ** -0.5)
    k = l2_norm(heads(qkv[..., MIX:2 * MIX]))
    v = heads(qkv[..., 2 * MIX:])
    beta = jax.nn.sigmoid(b_raw).transpose(0, 2, 1)
    log_a = (-jnp.exp(a_log_param.astype(jnp.float32))
             * jax.nn.softplus(a_raw + dt_bias.astype(jnp.float32))).transpose(0, 2, 1)
    o = gated_delta_rule(q, k, v, beta, log_a).transpose(0, 2, 1, 3)
    zg = jax.nn.silu(z.astype(jnp.float32).reshape(b, s, N_HEADS, HEAD_DIM))
    o = (rms_norm(o, norm_g) * zg).astype(hx.dtype).reshape(b, s, MIX)
    return o @ w_out


def fox_mixer(hx, w_in, b_f, q_norm_g, k_norm_g, w_out):
    b, s, _ = hx.shape
    proj = hx @ w_in
    heads = lambda t: t.reshape(b, s, N_HEADS, HEAD_DIM)
    q = rms_norm(heads(proj[..., :MIX]), q_norm_g).transpose(0, 2, 1, 3)
    k = rms_norm(heads(proj[..., MIX:2 * MIX]), k_norm_g).transpose(0, 2, 1, 3)
    v = heads(proj[..., 2 * MIX:3 * MIX]).transpose(0, 2, 1, 3)
    o_gate = proj[..., 3 * MIX:4 * MIX]
    log_f = jax.nn.log_sigmoid((proj[..., 4 * MIX:] + b_f).astype(jnp.float32))
    cum = jnp.cumsum(log_f, axis=1).transpose(0, 2, 1)
    nb = s // Q_BLOCK
    q_blocks = jnp.moveaxis(q.reshape(b, N_HEADS, nb, Q_BLOCK, HEAD_DIM), 2, 0)
    c_blocks = jnp.moveaxis(cum.reshape(b, N_HEADS, nb, Q_BLOCK), 2, 0)
    pos_k = jnp.arange(s)
    scale = HEAD_DIM ** -0.5

    def block(args):
        q_i, cq_i, i = args
        logits = (jnp.einsum('bhqd,bhkd->bhqk', q_i, k).astype(jnp.float32) * scale
                  + cq_i[..., :, None] - cum[:, :, None, :])
        pos_q = i * Q_BLOCK + jnp.arange(Q_BLOCK)
        logits = jnp.where(pos_k[None, :] <= pos_q[:, None], logits, -jnp.inf)
        p = jax.nn.softmax(logits, axis=-1)
        return jnp.einsum('bhqk,bhkd->bhqd', p.astype(v.dtype), v)

    o = lax.map(block, (q_blocks, c_blocks, jnp.arange(nb)))
    o = jnp.moveaxis(o, 0, 2).reshape(b, N_HEADS, s, HEAD_DIM).transpose(0, 2, 1, 3).reshape(b, s, MIX)
    return (o * jax.nn.sigmoid(o_gate)) @ w_out


def conv_ffn(hx, w_up, conv_w, w_down):
    u = causal_dwconv(hx @ w_up, conv_w)
    gate, val = jnp.split(u, 2, axis=-1)
    return (jax.nn.silu(gate) * val) @ w_down


def setup_inputs(seed: int = 0) -> dict:
    key = jax.random.key(seed)
    ks = jax.random.split(key, 24)
    d = D_MODEL
    nrm = lambda k, shape, sd: jax.random.normal(k, shape, jnp.float32) * sd
    x = nrm(ks[0], (BATCH, SEQ, d), 1.0)
    c = nrm(ks[1], (BATCH, d), 1.0)
    ada_w = nrm(ks[2], (DEPTH, d, 6 * d), 0.5 * d ** -0.5)
    ada_b = nrm(ks[3], (DEPTH, 6 * d), 0.02)
    norm_mix_g = 1.0 + nrm(ks[4], (DEPTH, d), 0.05)
    norm_ffn_g = 1.0 + nrm(ks[5], (DEPTH, d), 0.05)
    gdn_w_in = nrm(ks[6], (N_GDN, d, GDN_IN), d ** -0.5)
    gdn_conv_w = nrm(ks[7], (N_GDN, GDN_CONV, 3 * MIX), GDN_CONV ** -0.5)
    gdn_A_log = jnp.log(jax.random.uniform(ks[8], (N_GDN, N_HEADS), jnp.float32, 1.0, 16.0))
    dt = jnp.exp(jax.random.uniform(ks[9], (N_GDN, N_HEADS), jnp.float32, math.log(1e-3), math.log(1e-1)))
    gdn_dt_bias = dt + jnp.log(-jnp.expm1(-dt))
    gdn_norm_g = 1.0 + nrm(ks[10], (N_GDN, HEAD_DIM), 0.05)
    gdn_w_out = nrm(ks[11], (N_GDN, MIX, d), MIX ** -0.5)
    fox_w_in = nrm(ks[12], (N_FOX, d, FOX_IN), d ** -0.5)
    fox_b_f = jax.random.uniform(ks[13], (N_FOX, N_HEADS), jnp.float32, 1.0, 4.0)
    fox_q_norm_g = 1.0 + nrm(ks[14], (N_FOX, HEAD_DIM), 0.05)
    fox_k_norm_g = 1.0 + nrm(ks[15], (N_FOX, HEAD_DIM), 0.05)
    fox_w_out = nrm(ks[16], (N_FOX, MIX, d), MIX ** -0.5)
    ffn_w_up = nrm(ks[17], (DEPTH, d, 2 * FFN_DIM), d ** -0.5)
    ffn_conv_w = nrm(ks[18], (DEPTH, FFN_CONV, 2 * FFN_DIM), FFN_CONV ** -0.5)
    ffn_w_down = nrm(ks[19], (DEPTH, FFN_DIM, d), FFN_DIM ** -0.5)
    final_norm_g = 1.0 + nrm(ks[20], (d,), 0.05)
    return {'x': x, 'c': c, 'ada_w': ada_w, 'ada_b': ada_b,
            'norm_mix_g': norm_mix_g, 'norm_ffn_g': norm_ffn_g,
            'gdn_w_in': gdn_w_in, 'gdn_conv_w': gdn_conv_w, 'gdn_A_log': gdn_A_log,
            'gdn_dt_bias': gdn_dt_bias, 'gdn_norm_g': gdn_norm_g, 'gdn_w_out': gdn_w_out,
            'fox_w_in': fox_w_in, 'fox_b_f': fox_b_f, 'fox_q_norm_g': fox_q_norm_g,
            'fox_k_norm_g': fox_k_norm_g, 'fox_w_out': fox_w_out,
            'ffn_w_up': ffn_w_up, 'ffn_conv_w': ffn_conv_w, 'ffn_w_down': ffn_w_down,
            'final_norm_g': final_norm_g}


def reference(x, c, ada_w, ada_b, norm_mix_g, norm_ffn_g,
              gdn_w_in, gdn_conv_w, gdn_A_log, gdn_dt_bias, gdn_norm_g, gdn_w_out,
              fox_w_in, fox_b_f, fox_q_norm_g, fox_k_norm_g, fox_w_out,
              ffn_w_up, ffn_conv_w, ffn_w_down, final_norm_g):
    c_act = jax.nn.silu(c)
    for i in range(DEPTH):
        j = i // N_MIXERS
        mod = (c_act @ ada_w[i] + ada_b[i])[:, None, :]
        sh1, sc1, g1, sh2, sc2, g2 = jnp.split(mod, 6, axis=-1)
        h = rms_norm(x, norm_mix_g[i]) * (1.0 + sc1) + sh1
        if i % N_MIXERS == 0:
            mixed = gdn_mixer(h, gdn_w_in[j], gdn_conv_w[j], gdn_A_log[j], gdn_dt_bias[j],
                              gdn_norm_g[j], gdn_w_out[j])
        else:
            mixed = fox_mixer(h, fox_w_in[j], fox_b_f[j], fox_q_norm_g[j], fox_k_norm_g[j], fox_w_out[j])
        x = x + g1 * mixed
        h = rms_norm(x, norm_ffn_g[i]) * (1.0 + sc2) + sh2
        x = x + g2 * conv_ffn(h, ffn_w_up[i], ffn_conv_w[i], ffn_w_down[i])
    return rms_norm(x, final_norm_g)
```

```python
import numpy as np
import ml_dtypes
from contextlib import ExitStack
import concourse.bass as bass
import concourse.mybir as mybir
from concourse.bass_utils import run_bass_kernel_spmd

F32 = mybir.dt.float32
BF16 = mybir.dt.bfloat16
AF = mybir.ActivationFunctionType
ALU = mybir.AluOpType
AX = mybir.AxisListType
NPBF = ml_dtypes.bfloat16

G4 = [[0, 1, 2, 3], [4, 5, 6, 7]]
G2 = [[0, 4], [1, 5], [2, 6], [3, 7]]
BIG = 30000.0
EPS = 1e-6


class Cfg:
    def __init__(self, D=4096, S=4096, dbg=()):
        self.D, self.S, self.B, self.L = D, S, 2, 2
        self.H = D // 128
        self.KC = D // 128
        self.F = ((8 * D // 3 + 255) // 256) * 256
        self.FC = self.F // 128
        self.TL = S // 4
        self.HL = self.H // 4
        self.NB = S // 128
        self.E = D
        self.NG0 = D // 512
        self.TPL = 4 * self.NG0 + 3 * self.FC
        self.TPLP = ((self.TPL + 15) // 16) * 16
        self.NT = self.L * self.TPLP
        self.NR = self.NT // 16
        self.NLT = self.NT // 8
        self.NCH = 6 * self.KC // 8
        self.TP = min(512, self.TL)
        self.dbg = tuple(dbg)


_PSK = {'pg', 'pm', 'ph', 'npsn0', 'PA', 'PB', 'PC', 'PS', 'PO', 'PL'}


def _is_psum(k):
    return (isinstance(k, tuple) and k[0] in _PSK) or k == 'pm'


class Sched:
    def __init__(self, nc, es):
        self.nc, self.es = nc, es
        self.eng = {'pe': nc.tensor, 'act': nc.scalar, 'dve': nc.vector, 'pool': nc.gpsimd, 'sp': nc.sync}
        self.prog = {e: [] for e in self.eng}
        self.sems, self.cnt = {}, {}
        self.waited = {e: {} for e in self.eng}
        self.lastw, self.rd = {}, {}
        self.nops = 0

    def _sem(self, stream):
        if stream not in self.sems:
            self.sems[stream] = self.es.enter_context(self.nc.semaphore('S_' + str(stream)))
            self.cnt[stream] = 0
        return self.sems[stream]

    def _emit_waits(self, eng, deps):
        for s, v in deps.items():
            if self.waited[eng].get(s, 0) >= v:
                continue
            if s == eng and (eng == 'pe' or v > self.cnt[eng]):
                continue
            self.waited[eng][s] = v
            sem = self._sem(s)
            self.prog[eng].append(lambda E, sem=sem, v=v: E.wait_ge(sem, v))

    def _waits(self, eng, reads, writes):
        deps = {}

        def add(s, v):
            if v > deps.get(s, 0):
                deps[s] = v
        for k in reads:
            if k in self.lastw:
                add(*self.lastw[k])
            if _is_psum(k):
                for (s, v) in self.rd.get(k, ()):
                    if s != eng:
                        add(s, v)
        for k in writes:
            if k in self.lastw:
                add(*self.lastw[k])
            for (s, v) in self.rd.get(k, ()):
                add(s, v)
        self._emit_waits(eng, deps)

    def _record(self, stream, val, reads, writes):
        for k in writes:
            self.lastw[k] = (stream, val)
            self.rd[k] = []
        for k in reads:
            self.rd.setdefault(k, []).append((stream, val))

    def op(self, eng, fn, reads=(), writes=(), inc=True):
        self.nops += 1
        self._waits(eng, reads, writes)
        sem = self._sem(eng)
        val = self.cnt[eng] + 1
        if inc:
            self.cnt[eng] = val
            self.prog[eng].append(lambda E, fn=fn, sem=sem: fn(E).then_inc(sem, 1))
        else:
            self.prog[eng].append(lambda E, fn=fn: fn(E))
        self._record(eng, val, reads, writes)

    def dma(self, q, out, in_, reads=(), writes=(), stream=None):
        self.nops += 1
        if stream is None:
            stream = 'd_' + str(writes[0] if writes else reads[0])
        self._waits(q, reads, writes)
        sem = self._sem(stream)
        self.cnt[stream] += 16
        val = self.cnt[stream]
        self.prog[q].append(lambda E, sem=sem, out=out, in_=in_: E.dma_start(out=out, in_=in_).then_inc(sem, 16))
        self._record(stream, val, reads, writes)

    def cc(self, kind, groups, in_, out, reads=(), writes=()):
        self._waits('pool', reads, writes)
        sem = self._sem('cc')
        self.cnt['cc'] += 1
        val = self.cnt['cc']
        op = ALU.bypass if kind == 'AllGather' else ALU.add
        self.prog['pool'].append(lambda E, sem=sem, in_=in_, out=out: E.collective_compute(
            kind, op, replica_groups=groups, ins=[in_], outs=[out]).then_inc(sem, 1))
        self.prog['pool'].append(lambda E, sem=sem, val=val: E.wait_ge(sem, val))
        self.waited['pool']['cc'] = val
        self._record('cc', val, reads, writes)

    def barrier(self):
        deps = {s: v for s, v in self.cnt.items() if v > 0}
        for e in self.eng:
            self._emit_waits(e, dict(deps))

    def emit(self):
        prog = self.prog
        with self.nc.Block() as block:
            @block.sync
            def _(E):
                for f in prog['sp']:
                    f(E)

            @block.tensor
            def _(E):
                for f in prog['pe']:
                    f(E)

            @block.scalar
            def _(E):
                for f in prog['act']:
                    f(E)

            @block.vector
            def _(E):
                for f in prog['dve']:
                    f(E)

            @block.gpsimd
            def _(E):
                for f in prog['pool']:
                    f(E)
        self.prog = {e: [] for e in self.eng}


def interleave(gens):
    gens = list(gens)
    while gens:
        alive = []
        for g in gens:
            try:
                next(g)
                alive.append(g)
            except StopIteration:
                pass
        gens = alive


def _owner(t):
    r, q, i = t % 4, (t // 4) % 4, t // 16
    d, hf = q // 2, q % 2
    return 4 * d + r, 2 * i + hf


def prep_inputs(cfg, inp):
    D, S, KC, F, FC, E, H, HL, TL, L = cfg.D, cfg.S, cfg.KC, cfg.F, cfg.FC, cfg.E, cfg.H, cfg.HL, cfg.TL, cfg.L
    MIX = D
    f32 = lambda a: np.ascontiguousarray(np.asarray(a, dtype=np.float32))
    TPLP = cfg.TPLP
    tiles = np.zeros((cfg.NT, 128, E), np.float32)
    for l in range(L):
        t = l * TPLP
        mixer_wout = inp['gdn_w_out'][0] if l == 0 else inp['fox_w_out'][0]
        wo = f32(mixer_wout).reshape(KC, 128, cfg.NG0, 512).transpose(2, 1, 0, 3)
        tiles[t:t + 4 * cfg.NG0] = wo.reshape(cfg.NG0, 128, 4, E).transpose(0, 2, 1, 3).reshape(-1, 128, E)
        t += 4 * cfg.NG0
        wu = f32(inp['ffn_w_up'][l]).reshape(KC, 128, 2, FC, 128).transpose(3, 1, 0, 2, 4)
        tiles[t:t + 2 * FC] = wu.reshape(FC, 128, 2, E).transpose(0, 2, 1, 3).reshape(-1, 128, E)
        t += 2 * FC
        tiles[t:t + FC] = f32(inp['ffn_w_down'][l]).reshape(FC, 128, D)
    own = [[None] * cfg.NLT for _ in range(8)]
    for t in range(cfg.NT):
        c, lt = _owner(t)
        own[c][lt] = t
    consts = np.zeros((128, 6, 128), np.float32)
    ii, jj = np.meshgrid(np.arange(128), np.arange(128), indexing='ij')
    consts[:, 0] = (ii == jj)
    consts[:, 1] = (ii <= jj)
    consts[:, 2] = np.where(jj < ii, 0.0, BIG)
    consts[:, 3] = np.where(jj >= ii, 0.0, -BIG)
    consts[:, 4] = 1.0
    consts[:, 5] = np.where(jj <= ii, 0.0, BIG)
    x = f32(inp['x'])
    c_in = f32(inp['c'])
    cT = np.ascontiguousarray(c_in.T.reshape(KC, 128, 2).transpose(1, 0, 2)).reshape(128, KC * 2)
    ng = np.stack([f32(inp['norm_mix_g'][0]), f32(inp['norm_ffn_g'][0]), f32(inp['norm_mix_g'][1]),
                   f32(inp['norm_ffn_g'][1]), f32(inp['final_norm_g'])], 0)
    ng = np.ascontiguousarray(ng.reshape(5, KC, 128).transpose(2, 0, 1)).reshape(128, 5 * KC)
    w_in = [f32(inp['gdn_w_in'][0]), f32(inp['fox_w_in'][0])]
    ncol = 6 * D // 8
    maps = []
    for c in range(8):
        d, r = c // 4, c % 4
        m = {}
        xt = np.zeros((D, 4 + TL), np.float32)
        lo = r * TL - 4
        if r == 0:
            xt[:, 4:] = x[d, 0:TL].T
        else:
            xt[:] = x[d, lo:lo + TL + 4].T
        m['xT'] = xt
        m['cT'] = cT
        q = 2 * r + d
        m['adaw'] = np.ascontiguousarray(f32(inp['ada_w'])[:, :, q * ncol:(q + 1) * ncol]).reshape(L * D, ncol)
        ab = f32(inp['ada_b'])[:, q * ncol:(q + 1) * ncol].reshape(L, cfg.NCH, 128)
        m['adab'] = np.ascontiguousarray(ab.transpose(2, 0, 1)).reshape(128, L * cfg.NCH)
        m['ng'] = ng
        m['consts'] = consts.reshape(128, 6 * 128)
        m['wloc'] = tiles[own[c]].reshape(cfg.NLT * 128, E)
        wl = np.zeros((L, HL, 2, 128, E), np.float32)
        wg = np.zeros((128, L, KC, 2 * HL), np.float32)
        for l in range(L):
            for hl in range(HL):
                hd = r * HL + hl
                cols = np.concatenate([np.arange(s * MIX + hd * 128, s * MIX + hd * 128 + 128) for s in range(4)])
                blk = w_in[l][:, cols].reshape(KC, 128, 512).transpose(1, 0, 2).reshape(128, 4, E)
                wl[l, hl] = blk[:, 2 * d:2 * d + 2].transpose(1, 0, 2)
            hd0 = r * HL
            if l == 0:
                gc = np.concatenate([np.arange(4 * MIX + hd0, 4 * MIX + hd0 + HL),
                                     np.arange(4 * MIX + H + hd0, 4 * MIX + H + hd0 + HL)])
            else:
                gc = np.arange(4 * MIX + hd0, 4 * MIX + hd0 + HL)
            wg[:, l, :, :len(gc)] = w_in[l][:, gc].reshape(KC, 128, len(gc)).transpose(1, 0, 2)
        m['winloc'] = wl.reshape(L * HL * 2 * 128, E)
        m['wg'] = wg.reshape(128, L * KC * 2 * HL)
        hs = slice(r * HL, (r + 1) * HL)
        cw = f32(inp['gdn_conv_w'][0]).reshape(4, 3, H, 128)[:, :, hs]
        m['gconv'] = np.ascontiguousarray(cw.transpose(3, 2, 1, 0)).reshape(128, HL * 3 * 4)
        rowrep = lambda v: np.ascontiguousarray(np.broadcast_to(f32(v)[None, :], (128, len(v))))
        m['hrow'] = np.concatenate([rowrep(inp['gdn_A_log'][0][hs]), rowrep(inp['gdn_dt_bias'][0][hs]),
                                    rowrep(inp['fox_b_f'][0][hs])], 1)
        m['gnorm'] = rowrep(inp['gdn_norm_g'][0])
        m['fqk'] = np.stack([f32(inp['fox_q_norm_g'][0]), f32(inp['fox_k_norm_g'][0])], 1)
        fc = np.stack([f32(inp['ffn_conv_w'][l]).reshape(3, 2 * FC, 128) for l in range(L)], 0)
        m['fcw'] = np.ascontiguousarray(fc.transpose(3, 0, 2, 1)).reshape(128, L * 2 * FC * 3)
        sel = np.zeros((128, 8), np.float32)
        sel[:, d] = 1.0
        sel[:, 2 + r] = 1.0
        sel[:, 6] = 0.0 if r == 0 else 1.0
        m['sel'] = sel
        maps.append(m)
    return maps


def build(cfg):
    D, S, KC, F, FC, E, H, HL, TL, L, NB = cfg.D, cfg.S, cfg.KC, cfg.F, cfg.FC, cfg.E, cfg.H, cfg.HL, cfg.TL, cfg.L, cfg.NB
    NCH, NCOL, NLT, NR, TPLP, NG0, TP = cfg.NCH, 6 * D // 8, cfg.NLT, cfg.NR, cfg.TPLP, cfg.NG0, cfg.TP
    nc = bass.Bass("TRN2", target_bir_lowering=False)
    din = lambda name, shape, dt=F32: nc.dram_tensor(name, shape, dt, kind="ExternalInput")
    xT = din('xT', [D, 4 + TL]); cT = din('cT', [128, KC * 2]); adaw = din('adaw', [L * D, NCOL])
    adab = din('adab', [128, L * NCH]); ng = din('ng', [128, 5 * KC]); consts = din('consts', [128, 768])
    wloc = din('wloc', [NLT * 128, E]); winloc = din('winloc', [L * HL * 256, E]); wg = din('wg', [128, L * KC * 2 * HL])
    gconv = din('gconv', [128, HL * 12]); hrow = din('hrow', [128, 3 * HL]); gnorm = din('gnorm', [128, 128])
    fqk = din('fqk', [128, 2]); sel = din('sel', [128, 8]); fcw = din('fcw', [128, L * 2 * FC * 3])
    outT = nc.dram_tensor('outT', [D, TL], F32, kind="ExternalOutput")
    dbg = {}
    for name, shape, dt in cfg.dbg:
        dbg[name] = nc.dram_tensor('dbg_' + name, shape, dt, kind="ExternalOutput")
    wbf = nc.dram_tensor('wbf', [NLT * 128, E], BF16)
    Pb = [nc.dram_tensor(f'Pb{i}', [512, E], BF16) for i in range(NR)]
    Gb = [nc.dram_tensor(f'Gb{i}', [2048, E], BF16) for i in range(NR)]
    winbf = nc.dram_tensor('winbf', [L * HL * 256, E], BF16)
    WIN = [[nc.dram_tensor(f'WIN{l}_{hl}', [512, E], BF16) for hl in range(HL)] for l in range(L)]
    modd = nc.dram_tensor('modd', [128, L * NCH * 2], F32)
    modP = nc.dram_tensor('modP', [256, L * NCH * 2], F32)
    modG = nc.dram_tensor('modG', [1024, L * NCH * 2], F32)

    top = ExitStack()
    with top:
        Sd = Sched(nc, top)
        uid = [0]

        def sbt(es, name, shape, dt=F32):
            uid[0] += 1
            return es.enter_context(nc.sbuf_tensor(f'{name}_{uid[0]}', shape, dt))

        def pst(es, name, shape, dt=F32):
            uid[0] += 1
            return es.enter_context(nc.psum_tensor(f'{name}_{uid[0]}', shape, dt))
        CON = sbt(top, 'CON', [128, 6, 128])
        SEL = sbt(top, 'SEL', [128, 8])
        NGs = sbt(top, 'NGs', [128, 5, KC])
        MODL = sbt(top, 'MODL', [128, L, 6 * KC])
        Sd.dma('sp', CON[:].rearrange("p a b -> p (a b)"), consts[:, :], writes=['CON'])
        Sd.dma('sp', SEL[:], sel[:, :], writes=['SEL'])
        Sd.dma('sp', NGs[:].rearrange("p a b -> p (a b)"), ng[:, :], writes=['NGs'])
        IDENT, UT, POSS, NEGT, ONES, POSI = (CON[:, i, :] for i in range(6))

        def cast_win(l):
            pass

        def gather_win(l):
            for hl in range(HL):
                u = l * HL + hl
                Sd.dma('pool', winbf[u * 256:(u + 1) * 256, :], winloc[u * 256:(u + 1) * 256, :], writes=[('winbf', u)], stream='cast')
                Sd.cc('AllGather', G2, winbf[u * 256:(u + 1) * 256, :], WIN[l][hl][:, :],
                      reads=[('winbf', u)], writes=[('WIN', l, hl)])

        def gather_round(i):
            Sd.dma('pool', wbf[i * 256:(i + 1) * 256, :], wloc[i * 256:(i + 1) * 256, :], writes=[('wbf', i)], stream='cast')
            Sd.cc('AllGather', G2, wbf[i * 256:(i + 1) * 256, :], Pb[i][:, :], reads=[('wbf', i)], writes=[('Pb', i)])
            for q in range(4):
                Sd.cc('AllGather', G4, Pb[i][q * 128:(q + 1) * 128, :], Gb[i][q * 512:(q + 1) * 512, :],
                      reads=[('Pb', i)], writes=[('Gb', i)])

        RPL = TPLP // 16
        pend = list(range(NR))

        def drain(n):
            for _ in range(min(n, len(pend))):
                gather_round(pend.pop(0))

        def drain_to(i_end):
            while pend and pend[0] < i_end:
                gather_round(pend.pop(0))

        def gtile(t, n=1):
            i, o = t // 16, t % 16
            assert o + n <= 16
            return Gb[i][o * 128:(o + n) * 128, :].rearrange("(t p) e -> p t e", p=128), ('Gb', i)

        cast_win(0)
        gather_win(0)
        with ExitStack() as ph:
            cact = sbt(ph, 'cact', [128, KC, 2])
            wA = [sbt(ph, f'wA{i}', [128, NCOL]) for i in range(4)]
            adb = sbt(ph, 'adb', [128, L * NCH])
            modloc = sbt(ph, 'modloc', [128, L * NCH, 2])
            MODg = sbt(ph, 'MODg', [128, 8, L * NCH, 2])
            pm = pst(ph, 'pm', [128, L * NCH, 2])
            Sd.dma('sp', cact[:].rearrange("p a b -> p (a b)"), cT[:, :], writes=['cact'])
            Sd.dma('sp', adb[:], adab[:, :], writes=['adb'])
            Sd.op('act', lambda E_: E_.activation(cact[:], cact[:], AF.Silu), reads=['cact'], writes=['cact'])
            groups = [(c0, min(512, NCOL - c0)) for c0 in range(0, NCOL, 512)]
            pgs = [pst(ph, f'pg{i}', [128, 512]) for i in range(len(groups))]
            modrow = sbt(ph, 'modrow', [2, NCOL])
            for l in range(L):
                for kc in range(KC):
                    b = (l * KC + kc) % 4
                    Sd.dma('sp' if b % 2 == 0 else 'act', wA[b][:], adaw[l * D + kc * 128:l * D + (kc + 1) * 128, :], writes=[('wA', b)])
                    for gi, (c0, n) in enumerate(groups):
                        Sd.op('pe', lambda E_, b=b, gi=gi, c0=c0, n=n, kc=kc: E_.matmul(
                            pgs[gi][0:2, 0:n], cact[:, kc, :], wA[b][:, c0:c0 + n], start=(kc == 0), stop=(kc == KC - 1)),
                            reads=[('wA', b), 'cact'], writes=[('pg', gi)], inc=(gi == len(groups) - 1))
                for gi, (c0, n) in enumerate(groups):
                    Sd.op('act', lambda E_, gi=gi, c0=c0, n=n: E_.activation(modrow[0:2, c0:c0 + n], pgs[gi][0:2, 0:n], AF.Copy),
                          reads=[('pg', gi)], writes=['modrow'])
                for n_ in range(NCH):
                    Sd.op('pe', lambda E_, n_=n_, l=l: E_.transpose(pm[:, l * NCH + n_, :], modrow[0:2, n_ * 128:(n_ + 1) * 128], IDENT[0:2, 0:2]),
                          reads=['modrow', 'CON'], writes=['pm'], inc=(n_ == NCH - 1))
            for b in range(2):
                Sd.op('dve', lambda E_, b=b: E_.tensor_tensor(modloc[:, :, b], pm[:, :, b], adb[:], ALU.add),
                      reads=['pm', 'adb'], writes=['modloc'])
            Sd.dma('sp', modd[:, :], modloc[:].rearrange("p a b -> p (a b)"), reads=['modloc'], writes=['modd'])
            Sd.cc('AllGather', G2, modd[:, :], modP[:, :], reads=['modd'], writes=['modP'])
            Sd.cc('AllGather', G4, modP[:, :], modG[:, :], reads=['modP'], writes=['modG'])
            Sd.dma('sp', MODg[:].rearrange("p q a b -> p q (a b)"), modG[:, :].rearrange("(q p) f -> p q f", p=128),
                   reads=['modG'], writes=['MODg'])
            for l in range(L):
                ov = MODL[:, l, :].rearrange("p (q n) -> p q n", q=8)
                Sd.op('dve', lambda E_, l=l, ov=ov: E_.tensor_scalar(
                    ov, MODg[:, :, l * NCH:(l + 1) * NCH, 0], SEL[:, 0:1], None, op0=ALU.mult),
                    reads=['MODg', 'SEL'], writes=['MODL'])
                Sd.op('dve', lambda E_, l=l, ov=ov: E_.scalar_tensor_tensor(
                    ov, MODg[:, :, l * NCH:(l + 1) * NCH, 1], SEL[:, 1:2], ov, op0=ALU.mult, op1=ALU.add),
                    reads=['MODg', 'SEL', 'MODL'], writes=['MODL'])
            if 'modl' in dbg:
                Sd.dma('sp', dbg['modl'][:, :], MODL[:].rearrange("p a b -> p (a b)"), reads=['MODL'], stream='dbg')
            Sd.barrier()
            Sd.emit()

        DER = sbt(top, 'DER', [128, L, 2, KC])
        FCW = sbt(top, 'FCW', [128, L, 2 * FC, 3])
        Sd.dma('sp', FCW[:].rearrange("p a b c -> p (a b c)"), fcw[:, :], writes=['FCW'])
        for l in range(L):
            for j, (part, gi) in enumerate(((1, 2 * l), (4, 2 * l + 1))):
                Sd.op('dve', lambda E_, l=l, j=j, part=part, gi=gi: E_.scalar_tensor_tensor(
                    DER[:, l, j, :], MODL[:, l, part * KC:(part + 1) * KC], 1.0, NGs[:, gi, :], op0=ALU.add, op1=ALU.mult),
                    reads=['MODL', 'NGs'], writes=['DER'])
        modv = lambda l, part: MODL[:, l, part * KC:(part + 1) * KC]

        NHC = max(1, (D * TL * 2) // (1 << 20))
        RW = D // NHC
        hs = nc.dram_tensor('hs', [D, TL], BF16)
        HG = [nc.dram_tensor(f'HG{j}', [4 * RW, TL], BF16) for j in range(NHC)]
        xbuf1 = nc.dram_tensor('xbuf1', [D, 4 + TL], F32)
        oloc = [nc.dram_tensor(f'oloc{hl}', [128, S], BF16) for hl in range(HL)]
        OG = [nc.dram_tensor(f'OG{hl}', [512, S], BF16) for hl in range(HL)]

        def gather_h(gen):
            for j in range(NHC):
                Sd.cc('AllGather', G4, hs[j * RW:(j + 1) * RW, :], HG[j][:, :], reads=['hs'], writes=[('HG', j)])

        def norm_scratch(ph, tag, nps, npk, tmp=None, rstd=None):
            ns = {'tag': tag, 'nps': nps, 'npk': npk}
            ns['sq'] = [sbt(ph, f'sq{tag}{i}', [128, 516], BF16) for i in range(2)]
            ns['tmp'] = tmp if tmp is not None else [sbt(ph, f'tmpn{tag}{i}', [128, 516]) for i in range(2)]
            ns['tmpk'] = [('Ub', 0), ('Ub', 1)] if tmp is not None else [('tmpn', tag, 0), ('tmpn', tag, 1)]
            ns['rstd'] = rstd[0] if rstd is not None else sbt(ph, f'rstd{tag}', [128, 516])
            ns['rk'] = rstd[1] if rstd is not None else ('rstd', tag)
            ns['ones'] = sbt(ph, f'onesb{tag}', [128, 128], BF16)
            Sd.op('dve', lambda E_: E_.tensor_copy(ns['ones'][:], ONES), reads=['CON'], writes=[('onesb', tag)])
            return ns

        def norm_to(ns, xm, c0, w, Avec, Bvec, out_fn):
            tag, sq, tmp, rstd, ones_b, nps = ns['tag'], ns['sq'], ns['tmp'], ns['rstd'], ns['ones'], ns['nps']
            segs = [(0, w)] if w <= 512 else [(w - 512, 512), (0, w - 512)]
            for si, (a, n) in enumerate(segs):
                for kc in range(KC):
                    b = kc % 2
                    Sd.op('act', lambda E_, kc=kc, b=b, a=a, n=n: E_.activation(sq[b][:, 0:n], xm[:, kc, c0 + a:c0 + a + n], AF.Square),
                          reads=[('xm', kc)], writes=[('sq', tag, b)])
                    Sd.op('pe', lambda E_, kc=kc, b=b, si=si, n=n: E_.matmul(nps[si][:, 0:n], ones_b[:], sq[b][:, 0:n],
                                                                     start=(kc == 0), stop=(kc == KC - 1)),
                          reads=[('sq', tag, b), ('onesb', tag)], writes=[ns['npk'][si]])
                Sd.op('dve', lambda E_, si=si, a=a, n=n: E_.tensor_scalar(rstd[:, a:a + n], nps[si][:, 0:n], 1.0 / D, EPS, op0=ALU.mult, op1=ALU.add),
                      reads=[ns['npk'][si]], writes=[ns['rk']])
            Sd.op('act', lambda E_: E_.activation(rstd[:, 0:w], rstd[:, 0:w], AF.Sqrt), reads=[ns['rk']], writes=[ns['rk']])
            Sd.op('dve', lambda E_: E_.reciprocal(rstd[:, 0:w], rstd[:, 0:w]), reads=[ns['rk']], writes=[ns['rk']])
            for kc in range(KC):
                b = kc % 2
                oap, okeys = out_fn(kc)
                Sd.op('dve', lambda E_, kc=kc, b=b: E_.tensor_tensor(tmp[b][:, 0:w], xm[:, kc, c0:c0 + w], rstd[:, 0:w], ALU.mult),
                      reads=[('xm', kc), ns['rk']], writes=[ns['tmpk'][b]])
                if Bvec is not None:
                    Sd.op('act', lambda E_, kc=kc, b=b, oap=oap: E_.activation(oap, tmp[b][:, 0:w], AF.Identity,
                                                                        bias=Bvec[:, kc:kc + 1], scale=Avec[:, kc:kc + 1]),
                          reads=[ns['tmpk'][b], 'MODL', 'DER', 'NGs'], writes=okeys)
                else:
                    Sd.op('act', lambda E_, kc=kc, b=b, oap=oap: E_.activation(oap, tmp[b][:, 0:w], AF.Copy, scale=Avec[:, kc:kc + 1]),
                          reads=[ns['tmpk'][b], 'NGs'], writes=okeys)

        def phase_n0():
            with ExitStack() as ph:
                xm = sbt(ph, 'xm', [128, KC, 516])
                hb = sbt(ph, 'hb', [128, KC, 516], BF16)
                ns = norm_scratch(ph, 'n0', [pst(ph, 'npsn0', [128, 512])], [('npsn0',)])
                for p in range(TL // TP):
                    t0 = p * TP
                    Sd.dma('sp', xm[:, :, 0:TP], xT[:, 4 + t0:4 + t0 + TP].rearrange("(k p) t -> p k t", p=128), writes=[('xm', kc) for kc in range(KC)], stream='ld_xm')
                    norm_to(ns, xm, 0, TP, DER[:, 0, 0, :], modv(0, 0), lambda kc: (hb[:, kc, 0:TP], [('hb', kc)]))
                    Sd.dma('act', hs[:, t0:t0 + TP].rearrange("(k p) t -> p k t", p=128), hb[:, :, 0:TP], reads=[('hb', kc) for kc in range(KC)], writes=['hs'], stream='st_hs')
                gather_h(0)
                Sd.barrier()
                Sd.emit()

        def phase_tok(l):
            last = (l == L - 1)
            ho = 0 if last else 2
            hu = ho + 2
            W = hu + TP
            xin = xT if l == 0 else xbuf1
            tbase = l * TPLP
            GD = 4
            if not last:
                drain(RPL)
            with ExitStack() as ph:
                xm = sbt(ph, 'xm', [128, KC, 516])
                hb = sbt(ph, 'hb', [128, KC, 516], BF16)
                cand = [sbt(ph, 'cand0', [128, 4, 4, 516], BF16)] * 2
                WB = sbt(ph, 'WB', [128, 4, 2 * E], BF16)
                Ub = [sbt(ph, f'Ub{i}', [128, 516]) for i in range(2)]
                cg = sbt(ph, 'cg', [128, 516]); cv = sbt(ph, 'cv', [128, 516])
                actb = [sbt(ph, f'actb{i}', [128, GD, 516], BF16) for i in range(2)]
                Usave = sbt(ph, 'Usave', [128, 2 * FC, 4])
                NPASS = TL // TP
                outf = Ub
                pmain = [pst(ph, f'pmain{i}', [128, 512]) for i in range(4)]
                phal = [pst(ph, f'phal{i}', [128, 512]) for i in range(4)]
                ns = norm_scratch(ph, f't{l}', [pmain[0], phal[0]], [('pm', 0), ('ph', 0)], tmp=Ub, rstd=(cg, 'cg'))
                nslot = [0]

                def wslot():
                    nslot[0] += 1
                    return nslot[0] % 4

                for p in range(TL // TP):
                    t0 = p * TP
                    Sd.dma('sp', xm[:, :, 0:W], xin[:, 4 + t0 - hu:4 + t0 + TP].rearrange("(k p) t -> p k t", p=128), writes=[('xm', kc) for kc in range(KC)], stream='ld_xm')
                    for hl in range(HL):
                        cb = cand[hl % 2]
                        for j in range(4):
                            c_lo = j * TL + t0 - hu
                            src = OG[hl][:, :].rearrange("(a p) t -> p a t", p=128)
                            if c_lo < 0:
                                Sd.dma('sp', cb[:, j, :, hu:W], src[:, :, 0:TP], reads=[('OG', hl)], writes=[('cand', 0)])
                                Sd.dma('sp', cb[:, j, :, 0:hu], src[:, :, 0:hu], reads=[('OG', hl)], writes=[('cand', 0)])
                            else:
                                Sd.dma('sp', cb[:, j, :, 0:W], src[:, :, c_lo:c_lo + W], reads=[('OG', hl)], writes=[('cand', 0)])
                        ov = hb[:].rearrange("p (a h) w -> p a h w", h=HL)[:, :, hl, 0:W]
                        okeys = [('hb', a * HL + hl) for a in range(4)]
                        Sd.op('dve', lambda E_, cb=cb, ov=ov: E_.tensor_scalar(ov, cb[:, 0, :, 0:W], SEL[:, 2:3], None, op0=ALU.mult),
                              reads=[('cand', 0), 'SEL'], writes=okeys)
                        for j in range(1, 4):
                            Sd.op('dve', lambda E_, cb=cb, ov=ov, j=j: E_.scalar_tensor_tensor(
                                ov, cb[:, j, :, 0:W], SEL[:, 2 + j:3 + j], ov, op0=ALU.mult, op1=ALU.add),
                                reads=[('cand', 0), 'SEL'] + okeys, writes=okeys)
                    g1 = modv(l, 2)
                    for g in range(NG0):
                        sl = wslot()
                        src, gk = gtile(tbase + 4 * g, 4)
                        sl2 = wslot()
                        Sd.dma('sp', WB[:, sl, :].rearrange("p (t e) -> p t e", t=2), src[:, 0:2, :], reads=[gk], writes=[('WB', sl)])
                        Sd.dma('sp', WB[:, sl2, :].rearrange("p (t e) -> p t e", t=2), src[:, 2:4, :], reads=[gk], writes=[('WB', sl2)])
                        for n4 in range(4):
                            n = g * 4 + n4
                            pi = 2 + n % 2
                            for (kind, c_a, c_n) in ((('m', hu, TP), ('h', 0, hu)) if p == 0 else (('m', hu, TP),)):
                                for kc in range(KC):
                                    wsl = sl if kc < KC // 2 else sl2
                                    wv = WB[:, wsl, :].rearrange("p (k j) -> p k j", j=512)[:, kc % (KC // 2), n4 * 128:(n4 + 1) * 128]
                                    out_ap = pmain[pi][:, 0:TP] if kind == 'm' else phal[pi][:, 0:hu]
                                    okey = ('pm', pi) if kind == 'm' else ('ph', pi)
                                    Sd.op('pe', lambda E_, wv=wv, out_ap=out_ap, kc=kc, c_a=c_a, c_n=c_n: E_.matmul(
                                        out_ap, wv, hb[:, kc, c_a:c_a + c_n], start=(kc == 0), stop=(kc == KC - 1)),
                                        reads=[('WB', wsl), ('hb', kc)], writes=[okey], inc=(kc == KC - 1))
                                Sd.op('dve', lambda E_, out_ap=out_ap, n=n, c_a=c_a, c_n=c_n: E_.scalar_tensor_tensor(
                                    xm[:, n, c_a:c_a + c_n], out_ap, g1[:, n:n + 1], xm[:, n, c_a:c_a + c_n], op0=ALU.mult, op1=ALU.add),
                                    reads=[okey, 'MODL', ('xm', n)], writes=[('xm', n)])
                    norm_to(ns, xm, 0, W, DER[:, l, 1, :], modv(l, 3), lambda kc: (hb[:, kc, 0:W], [('hb', kc)]))
                    if p == 0:
                        Sd.op('dve', lambda E_: E_.tensor_scalar(hb[:, :, 0:hu], hb[:, :, 0:hu], SEL[:, 6:7], None, op0=ALU.mult),
                              reads=[('hb', kc) for kc in range(KC)] + ['SEL'], writes=[('hb', kc) for kc in range(KC)])
                    g2 = modv(l, 5)
                    WO = TP + ho
                    xo = hu - ho

                    def down_group(grp, ab):
                        fcs = list(range(grp * GD, min(FC, (grp + 1) * GD)))
                        slots = []
                        for h2 in range(0, len(fcs), 2):
                            sub = fcs[h2:h2 + 2]
                            sl = wslot()
                            src, gk = gtile(tbase + 4 * NG0 + 2 * FC + sub[0], len(sub))
                            Sd.dma('sp', WB[:, sl, 0:len(sub) * E].rearrange("p (t e) -> p t e", e=E), src, reads=[gk], writes=[('WB', sl)])
                            slots.append(sl)
                        for dc in range(KC):
                            pi = 2 + dc % 2
                            for (kind, c_a, c_n) in (('m', ho, TP), ('h', 0, ho)):
                                if c_n == 0 or (kind == 'h' and p > 0):
                                    continue
                                out_ap = pmain[pi][:, 0:TP] if kind == 'm' else phal[pi][:, 0:ho]
                                okey = ('pm', pi) if kind == 'm' else ('ph', pi)
                                for i, fc in enumerate(fcs):
                                    sl = slots[i // 2]
                                    Sd.op('pe', lambda E_, out_ap=out_ap, i=i, dc=dc, c_a=c_a, c_n=c_n, sl=sl: E_.matmul(
                                        out_ap, WB[:, sl, (i % 2) * E + dc * 128:(i % 2) * E + (dc + 1) * 128], ab[:, i, c_a:c_a + c_n],
                                        start=(i == 0), stop=(i == len(fcs) - 1)),
                                        reads=[('WB', sl), ('actb', id(ab))], writes=[okey], inc=(i == len(fcs) - 1))
                                Sd.op('dve', lambda E_, out_ap=out_ap, dc=dc, c_a=c_a, c_n=c_n: E_.scalar_tensor_tensor(
                                    xm[:, dc, xo + c_a:xo + c_a + c_n], out_ap, g2[:, dc:dc + 1], xm[:, dc, xo + c_a:xo + c_a + c_n],
                                    op0=ALU.mult, op1=ALU.add), reads=[okey, 'MODL', ('xm', dc)], writes=[('xm', dc)])

                    pending = None
                    for j in range(FC):
                        grp, gi = j // GD, j % GD
                        ab = actb[grp % 2]
                        sl = wslot()
                        src, gk = gtile(tbase + 4 * NG0 + 2 * j, 2)
                        Sd.dma('sp', WB[:, sl, :].rearrange("p (t e) -> p t e", t=2), src, reads=[gk], writes=[('WB', sl)])
                        wv = WB[:, sl, :].rearrange("p (k c) -> p k c", c=256)
                        for gv in range(2):
                            ub = Ub[gv]
                            ch = gv * FC + j
                            for (kind, c_a, c_n) in ((('m', hu, TP), ('h', 0, hu)) if p == 0 else (('m', hu, TP),)):
                                out_ap = pmain[gv][:, 0:TP] if kind == 'm' else phal[gv][:, 0:hu]
                                okey = ('pm', gv) if kind == 'm' else ('ph', gv)
                                for kc in range(KC):
                                    Sd.op('pe', lambda E_, out_ap=out_ap, kc=kc, gv=gv, c_a=c_a, c_n=c_n, wv=wv: E_.matmul(
                                        out_ap, wv[:, kc, gv * 128:(gv + 1) * 128], hb[:, kc, c_a:c_a + c_n],
                                        start=(kc == 0), stop=(kc == KC - 1)),
                                        reads=[('WB', sl), ('hb', kc)], writes=[okey], inc=(kc == KC - 1))
                                Sd.op('act', lambda E_, out_ap=out_ap, ub=ub, c_a=c_a, c_n=c_n: E_.activation(ub[:, c_a:c_a + c_n], out_ap, AF.Copy),
                                      reads=[okey], writes=[('Ub', gv)])
                            if p > 0:
                                Sd.op('act', lambda E_, ub=ub, ch=ch: E_.activation(ub[:, 0:hu], Usave[:, ch, 4 - hu:4], AF.Copy),
                                      reads=[('Usave', ch)], writes=[('Ub', gv)])
                            if p < NPASS - 1:
                                Sd.op('act', lambda E_, ub=ub, ch=ch: E_.activation(Usave[:, ch, 4 - hu:4], ub[:, W - hu:W], AF.Copy),
                                      reads=[('Ub', gv)], writes=[('Usave', ch)])
                            dst = cg if gv == 0 else cv
                            dk = 'cg' if gv == 0 else 'cv'
                            Sd.op('dve', lambda E_, ub=ub, dst=dst, ch=ch: E_.tensor_scalar(
                                dst[:, 0:WO], ub[:, xo:xo + WO], FCW[:, l, ch, 2:3], None, op0=ALU.mult),
                                reads=[('Ub', gv), 'FCW'], writes=[dk])
                            for tap in (1, 0):
                                sh = 2 - tap
                                Sd.op('dve', lambda E_, ub=ub, dst=dst, ch=ch, tap=tap, sh=sh: E_.scalar_tensor_tensor(
                                    dst[:, 0:WO], ub[:, xo - sh:xo - sh + WO], FCW[:, l, ch, tap:tap + 1], dst[:, 0:WO],
                                    op0=ALU.mult, op1=ALU.add), reads=[('Ub', gv), 'FCW', dk], writes=[dk])
                        Sd.op('act', lambda E_: E_.activation(cg[:, 0:WO], cg[:, 0:WO], AF.Silu), reads=['cg'], writes=['cg'])
                        Sd.op('dve', lambda E_, ab=ab, gi=gi: E_.tensor_tensor(ab[:, gi, 0:WO], cg[:, 0:WO], cv[:, 0:WO], ALU.mult),
                              reads=['cg', 'cv'], writes=[('actb', id(ab))])
                        if gi == GD - 1 or j == FC - 1:
                            if pending is not None:
                                down_group(*pending)
                            pending = (grp, ab)
                    down_group(*pending)
                    if not last:
                        so = ho if p == 0 else 0
                        Sd.dma('act', xbuf1[:, 4 + t0 - so:4 + t0 + TP].rearrange("(k p) t -> p k t", p=128), xm[:, :, hu - so:hu + TP],
                               reads=[('xm', kc) for kc in range(KC)], writes=['xbuf1'], stream='st_x')
                        norm_to(ns, xm, hu, TP, DER[:, l + 1, 0, :], modv(l + 1, 0), lambda kc: (hb[:, kc, 0:TP], [('hb', kc)]))
                        Sd.dma('act', hs[:, t0:t0 + TP].rearrange("(k p) t -> p k t", p=128), hb[:, :, 0:TP], reads=[('hb', kc) for kc in range(KC)], writes=['hs'], stream='st_hs')
                    else:
                        def ofn(kc):
                            return outf[kc % 2][:, 0:TP], [('outf', kc % 2)]
                        tag = ns['tag']
                        norm_final(ns, xm, hu, TP, NGs[:, 4, :], outf, t0)
                if not last:
                    gather_h(l + 1)
                Sd.barrier()
                Sd.emit()

        def norm_final(ns, xm, c0, w, Avec, outf, t0):
            tag, sq, tmp, rstd, ones_b, nps = ns['tag'], ns['sq'], ns['tmp'], ns['rstd'], ns['ones'], ns['nps']
            for kc in range(KC):
                b = kc % 2
                Sd.op('act', lambda E_, kc=kc, b=b: E_.activation(sq[b][:, 0:w], xm[:, kc, c0:c0 + w], AF.Square),
                      reads=[('xm', kc)], writes=[('sq', tag, b)])
                Sd.op('pe', lambda E_, kc=kc, b=b: E_.matmul(nps[0][:, 0:w], ones_b[:], sq[b][:, 0:w], start=(kc == 0), stop=(kc == KC - 1)),
                      reads=[('sq', tag, b), ('onesb', tag)], writes=[ns['npk'][0]])
            Sd.op('dve', lambda E_: E_.tensor_scalar(rstd[:, 0:w], nps[0][:, 0:w], 1.0 / D, EPS, op0=ALU.mult, op1=ALU.add),
                  reads=[ns['npk'][0]], writes=[ns['rk']])
            Sd.op('act', lambda E_: E_.activation(rstd[:, 0:w], rstd[:, 0:w], AF.Sqrt), reads=[ns['rk']], writes=[ns['rk']])
            Sd.op('dve', lambda E_: E_.reciprocal(rstd[:, 0:w], rstd[:, 0:w]), reads=[ns['rk']], writes=[ns['rk']])
            for kc in range(KC):
                b = kc % 2
                Sd.op('dve', lambda E_, kc=kc, b=b: E_.scalar_tensor_tensor(outf[b][:, 0:w], xm[:, kc, c0:c0 + w], Avec[:, kc:kc + 1], rstd[:, 0:w],
                                                                    op0=ALU.mult, op1=ALU.mult),
                      reads=[('xm', kc), ns['rk'], 'NGs'], writes=[('Ub', b)])
                Sd.dma('act', outT[kc * 128:(kc + 1) * 128, t0:t0 + w], outf[b][:, 0:w], reads=[('Ub', b)], stream=f'st_out{b}')


        def load_h_tile(ht, key, c0):
            rank, col = c0 // TL, c0 % TL
            for j in range(NHC):
                a = RW // 128
                Sd.dma('sp', ht[:, j * a:(j + 1) * a, :], HG[j][rank * RW:(rank + 1) * RW, col:col + 512].rearrange("(a p) t -> p a t", p=128),
                       reads=[('HG', j)], writes=[key])

        def phase_gdn(l):
            NTT = S // 512
            with ExitStack() as ph:
                Wh = sbt(ph, 'Wh', [128, 4, E], BF16)
                Wg = sbt(ph, 'Wg', [128, KC, 2 * HL], BF16)
                Wgf = sbt(ph, 'Wgf', [128, KC, 2 * HL])
                ht = [sbt(ph, f'ht{i}', [128, KC, 512], BF16) for i in range(2)]
                GT = sbt(ph, 'GT', [128, NB, 2 * HL])
                BETA = sbt(ph, 'BETA', [128, HL, NB]); LA = sbt(ph, 'LA', [128, HL, NB])
                HR = sbt(ph, 'HR', [128, 3 * HL]); GCV = sbt(ph, 'GCV', [128, HL, 3, 4]); GN = sbt(ph, 'GN', [128, 128])
                IDB = sbt(ph, 'IDB', [128, 128], BF16); ONB = sbt(ph, 'ONB', [128, 128], BF16)
                gt = [sbt(ph, f'gtmp{i}', [128, NB]) for i in range(4)]
                nexpA = sbt(ph, 'nexpA', [128, HL])
                PA = [pst(ph, f'PA{i}', [128, 512]) for i in range(2)]
                PBk = [pst(ph, f'PB{i}', [128, 512]) for i in range(4)]
                PC = [pst(ph, f'PC{i}', [128, 512]) for i in range(2)]
                Sd.dma('sp', HR[:], hrow[:, :], writes=['HR'])
                Sd.dma('sp', GCV[:].rearrange("p a b c -> p (a b c)"), gconv[:, :], writes=['GCV'])
                Sd.dma('sp', GN[:], gnorm[:, :], writes=['GN'])
                Sd.dma('sp', Wgf[:].rearrange("p a b -> p (a b)"), wg[:, l * KC * 2 * HL:(l + 1) * KC * 2 * HL], writes=['Wgf'])
                Sd.op('dve', lambda E_: E_.tensor_copy(Wg[:], Wgf[:]), reads=['Wgf'], writes=['Wg'])
                Sd.op('dve', lambda E_: E_.tensor_copy(IDB[:], IDENT), reads=['CON'], writes=['IDB'])
                Sd.op('dve', lambda E_: E_.tensor_copy(ONB[:], ONES), reads=['CON'], writes=['ONB'])
                for tt in range(NTT):
                    hb_ = ht[tt % 2]
                    load_h_tile(hb_, ('ht', tt % 2), tt * 512)
                    for blk in range(4):
                        for kc in range(KC):
                            Sd.op('pe', lambda E_, hb_=hb_, kc=kc, blk=blk: E_.matmul(
                                PA[blk % 2][:, 0:2 * HL], hb_[:, kc, blk * 128:(blk + 1) * 128], Wg[:, kc, :],
                                start=(kc == 0), stop=(kc == KC - 1)), reads=[('ht', tt % 2), 'Wg'], writes=[('PA', blk % 2)], inc=(kc == KC - 1))
                        Sd.op('act', lambda E_, blk=blk, tt=tt: E_.activation(GT[:, tt * 4 + blk, :], PA[blk % 2][:, 0:2 * HL], AF.Copy),
                              reads=[('PA', blk % 2)], writes=['GT'])
                Sd.op('act', lambda E_: E_.activation(nexpA[:], HR[:, 0:HL], AF.Exp), reads=['HR'], writes=['nexpA'])
                Sd.op('dve', lambda E_: E_.tensor_scalar(nexpA[:], nexpA[:], -1.0, None, op0=ALU.mult), reads=['nexpA'], writes=['nexpA'])
                for hl in range(HL):
                    braw = GT[:, :, hl]; araw = GT[:, :, HL + hl]
                    Sd.op('act', lambda E_, hl=hl, braw=braw: E_.activation(BETA[:, hl, :], braw, AF.Sigmoid), reads=['GT'], writes=['BETA'])
                    Sd.op('dve', lambda E_, hl=hl, araw=araw: E_.tensor_scalar(gt[0][:], araw, HR[:, HL + hl:HL + hl + 1], None, op0=ALU.add),
                          reads=['GT', 'HR'], writes=['gt0'])
                    Sd.op('act', lambda E_: E_.activation(gt[1][:], gt[0][:], AF.Abs), reads=['gt0'], writes=['gt1'])
                    Sd.op('act', lambda E_: E_.activation(gt[1][:], gt[1][:], AF.Exp, scale=-1.0), reads=['gt1'], writes=['gt1'])
                    Sd.op('act', lambda E_: E_.activation(gt[1][:], gt[1][:], AF.Ln, bias=1.0), reads=['gt1'], writes=['gt1'])
                    Sd.op('dve', lambda E_: E_.scalar_tensor_tensor(gt[2][:], gt[0][:], 0.0, gt[1][:], op0=ALU.max, op1=ALU.add),
                          reads=['gt0', 'gt1'], writes=['gt2'])
                    Sd.op('dve', lambda E_, hl=hl: E_.tensor_scalar(LA[:, hl, :], gt[2][:], nexpA[:, hl:hl + 1], None, op0=ALU.mult),
                          reads=['gt2', 'nexpA'], writes=['LA'])
                if 'gates' in dbg:
                    Sd.dma('sp', dbg['gates'][:, 0:HL * NB], BETA[:].rearrange("p a b -> p (a b)"), reads=['BETA'], stream='dbg')
                    Sd.dma('sp', dbg['gates'][:, HL * NB:2 * HL * NB], LA[:].rearrange("p a b -> p (a b)"), reads=['LA'], stream='dbg2')
                pcb = [sbt(ph, f'pcb{s_}', [128, 515]) for s_ in range(3)]
                post = [sbt(ph, f'post{s_}', [128, 512]) for s_ in range(3)]
                sqb = sbt(ph, 'sqb', [128, 512], BF16); rs = sbt(ph, 'rs', [128, 512])
                zsf = sbt(ph, 'zsf', [128, 512])
                QT = [sbt(ph, f'QT{i}', [128, 512]) for i in range(2)]
                KT = [sbt(ph, f'KT{i}', [128, 512]) for i in range(2)]
                QTh = [sbt(ph, f'QTh{i}', [128, 512], BF16) for i in range(2)]
                KTh = [sbt(ph, f'KTh{i}', [128, 512], BF16) for i in range(2)]
                KTM = [sbt(ph, f'KTM{i}', [128, 4, 128]) for i in range(2)]
                VB = [sbt(ph, f'VB{i}', [128, 4, 128], BF16) for i in range(2)]
                ZS = [sbt(ph, f'ZS{i}', [128, 4, 128], BF16) for i in range(3)]
                def blkbufs(i):
                    d_ = {}
                    for nm in ('LAB', 'GROW', 'EGR', 'T1', 'DM', 'DTM', 'M0', 'M1', 'N0', 'N1', 'IM', 'R0', 'R1'):
                        d_[nm] = sbt(ph, f'{nm}_{i}', [128, 128])
                    for nm in ('KBG', 'RTh'):
                        d_[nm] = sbt(ph, f'{nm}_{i}', [128, 128], BF16)
                    d_['sc'] = sbt(ph, f'sc_{i}', [128, 8])
                    return d_
                BB = [blkbufs(i) for i in range(4)]
                def p3bufs(i):
                    d_ = {}
                    d_['U'] = sbt(ph, f'U_{i}', [128, 4, 128])
                    for nm in ('WT', 'QHT', 'ATT', 'KTL'):
                        d_[nm] = sbt(ph, f'{nm}_{i}', [128, 4, 128], BF16)
                    d_['EGL'] = sbt(ph, f'EGL_{i}', [128, 4])
                    return d_
                P3 = [p3bufs(i) for i in range(2)]
                Sst = sbt(ph, 'Sst', [128, 128]); VN = sbt(ph, 'VN', [128, 128], BF16); Sb = sbt(ph, 'Sb', [128, 128], BF16)
                ss = sbt(ph, 'ss', [128, 4]); junk = sbt(ph, 'junk', [128, 128])
                Y1 = sbt(ph, 'Y1', [128, 128]); Y2 = sbt(ph, 'Y2', [128, 128])
                OT = [sbt(ph, f'OT{i}', [128, 512], BF16) for i in range(2)]
                QSCALE = 128.0 ** -0.5

                def phase1(hl, tt):
                    b = tt % 2
                    hb_ = ht[b]
                    load_h_tile(hb_, ('ht', b), tt * 512)
                    wv = Wh[:].rearrange("p t (k c) -> p (t k) c", c=512)
                    for s_ in range(4):
                        pa = PA[s_ % 2]; pk = ('PA', s_ % 2)
                        for kc in range(KC):
                            Sd.op('pe', lambda E_, pa=pa, kc=kc, s_=s_: E_.matmul(pa[:, :], wv[:, kc, s_ * 128:(s_ + 1) * 128], hb_[:, kc, :],
                                                                         start=(kc == 0), stop=(kc == KC - 1)),
                                  reads=['Wh', ('ht', b)], writes=[pk], inc=(kc == KC - 1))
                            if kc % 4 == 3 and kc != KC - 1:
                                yield
                        if s_ < 3:
                            pc_, po_ = pcb[s_], post[s_]
                            Sd.op('act', lambda E_, pa=pa, pc_=pc_: E_.activation(pc_[:, 3:515], pa[:, :], AF.Copy), reads=[pk], writes=[('pcb', s_)])
                            Sd.op('dve', lambda E_, pc_=pc_, po_=po_, s_=s_: E_.tensor_scalar(po_[:], pc_[:, 3:515], GCV[:, hl, s_, 3:4], None, op0=ALU.mult),
                                  reads=[('pcb', s_), 'GCV'], writes=[('post', s_)])
                            for tap in (2, 1, 0):
                                Sd.op('dve', lambda E_, pc_=pc_, po_=po_, s_=s_, tap=tap: E_.scalar_tensor_tensor(
                                    po_[:], pc_[:, tap:tap + 512], GCV[:, hl, s_, tap:tap + 1], po_[:], op0=ALU.mult, op1=ALU.add),
                                    reads=[('pcb', s_), 'GCV', ('post', s_)], writes=[('post', s_)])
                            Sd.op('dve', lambda E_, pc_=pc_: E_.tensor_copy(pc_[:, 0:3], pc_[:, 512:515]), reads=[('pcb', s_)], writes=[('pcb', s_)])
                            Sd.op('act', lambda E_, po_=po_: E_.activation(po_[:], po_[:], AF.Silu), reads=[('post', s_)], writes=[('post', s_)])
                        if s_ < 2:
                            Sd.op('act', lambda E_, po_=po_: E_.activation(sqb[:], po_[:], AF.Square), reads=[('post', s_)], writes=['sqb'])
                            Sd.op('pe', lambda E_, pa=pa: E_.matmul(pa[:, :], ONB[:], sqb[:], start=True, stop=True), reads=['sqb', 'ONB'], writes=[pk])
                            Sd.op('dve', lambda E_, pa=pa: E_.tensor_scalar(rs[:], pa[:, :], EPS, None, op0=ALU.add), reads=[pk], writes=['rs'])
                            Sd.op('act', lambda E_: E_.activation(rs[:], rs[:], AF.Sqrt), reads=['rs'], writes=['rs'])
                            Sd.op('dve', lambda E_: E_.reciprocal(rs[:], rs[:]), reads=['rs'], writes=['rs'])
                            if s_ == 0:
                                Sd.op('dve', lambda E_, po_=po_: E_.scalar_tensor_tensor(QT[b][:], po_[:], QSCALE, rs[:], op0=ALU.mult, op1=ALU.mult),
                                      reads=[('post', 0), 'rs'], writes=[('QT', b)])
                                Sd.op('act', lambda E_: E_.activation(QTh[b][:], QT[b][:], AF.Copy), reads=[('QT', b)], writes=[('QTh', b)])
                            else:
                                Sd.op('dve', lambda E_, po_=po_: E_.tensor_tensor(KT[b][:], po_[:], rs[:], ALU.mult), reads=[('post', 1), 'rs'], writes=[('KT', b)])
                                Sd.op('act', lambda E_: E_.activation(KTh[b][:], KT[b][:], AF.Copy), reads=[('KT', b)], writes=[('KTh', b)])
                                for blk in range(4):
                                    Sd.op('pe', lambda E_, pa=pa, blk=blk: E_.transpose(pa[:, blk * 128:(blk + 1) * 128], KT[b][:, blk * 128:(blk + 1) * 128], IDENT),
                                          reads=[('KT', b), 'CON'], writes=[pk], inc=(blk == 3))
                                Sd.op('act', lambda E_, pa=pa: E_.activation(KTM[b][:].rearrange("p a c -> p (a c)"), pa[:, :], AF.Copy), reads=[pk], writes=[('KTM', b)])
                        if s_ == 2:
                            for blk in range(4):
                                Sd.op('pe', lambda E_, pa=pa, blk=blk, po_=po_: E_.transpose(pa[:, blk * 128:(blk + 1) * 128], po_[:, blk * 128:(blk + 1) * 128], IDENT),
                                      reads=[('post', 2), 'CON'], writes=[pk], inc=(blk == 3))
                            for blk in range(4):
                                Sd.op('act', lambda E_, pa=pa, blk=blk: E_.activation(VB[b][:, blk, :], pa[:, blk * 128:(blk + 1) * 128], AF.Copy,
                                                                              scale=BETA[:, hl, tt * 4 + blk:tt * 4 + blk + 1]),
                                      reads=[pk, 'BETA'], writes=[('VB', b)])
                        if s_ == 3:
                            Sd.op('act', lambda E_, pa=pa: E_.activation(zsf[:], pa[:, :], AF.Silu), reads=[pk], writes=['zsf'])
                            for blk in range(4):
                                Sd.op('pe', lambda E_, pa=pa, blk=blk: E_.transpose(pa[:, blk * 128:(blk + 1) * 128], zsf[:, blk * 128:(blk + 1) * 128], IDENT),
                                      reads=['zsf', 'CON'], writes=[pk], inc=(blk == 3))
                            Sd.op('act', lambda E_, pa=pa: E_.activation(ZS[tt % 3][:].rearrange("p a c -> p (a c)"), pa[:, :], AF.Copy), reads=[pk], writes=[('ZS', tt % 3)])
                        yield

                def phase2(hl, tt, blk):
                    b = tt % 2
                    B_ = BB[blk]; pb = PBk[blk]; pk = ('PB', blk); P_ = P3[b]
                    g = tt * 4 + blk
                    cs = slice(blk * 128, (blk + 1) * 128)
                    la = LA[:, hl, g:g + 1]; beta = BETA[:, hl, g:g + 1]
                    sc = B_['sc']
                    k = lambda nm: (nm, blk)
                    Sd.op('dve', lambda E_: E_.tensor_scalar(B_['LAB'][:], ONES, la, None, op0=ALU.mult), reads=['CON', 'LA'], writes=[k('LAB')])
                    Sd.op('pe', lambda E_: E_.matmul(pb[:, 0:128], B_['LAB'][:], UT, start=True, stop=True), reads=[k('LAB'), 'CON'], writes=[pk], inc=False)
                    Sd.op('pe', lambda E_: E_.matmul(pb[:, 128:129], UT, la, start=True, stop=True), reads=['LA', 'CON'], writes=[pk])
                    Sd.op('act', lambda E_: E_.activation(B_['GROW'][:], pb[:, 0:128], AF.Copy), reads=[pk], writes=[k('GROW')])
                    Sd.op('act', lambda E_: E_.activation(sc[:, 0:1], pb[:, 128:129], AF.Copy), reads=[pk], writes=[k('sc')])
                    yield
                    gcol = sc[:, 0:1]; glast = B_['GROW'][:, 127:128]
                    Sd.op('act', lambda E_: E_.activation(sc[:, 1:2], gcol, AF.Exp), reads=[k('sc')], writes=[k('sc')])
                    Sd.op('act', lambda E_: E_.activation(sc[:, 2:3], gcol, AF.Exp, scale=-1.0, bias=glast), reads=[k('sc'), k('GROW')], writes=[k('sc')])
                    Sd.op('act', lambda E_: E_.activation(P_['EGL'][:, blk:blk + 1], glast, AF.Exp), reads=[k('GROW')], writes=[('EGL', b)])
                    Sd.op('act', lambda E_: E_.activation(B_['EGR'][:], B_['GROW'][:], AF.Exp), reads=[k('GROW')], writes=[k('EGR')])
                    Sd.op('dve', lambda E_: E_.scalar_tensor_tensor(B_['T1'][:], B_['GROW'][:], gcol, POSS, op0=ALU.subtract, op1=ALU.max),
                          reads=[k('GROW'), k('sc'), 'CON'], writes=[k('T1')])
                    Sd.op('act', lambda E_: E_.activation(B_['DM'][:], B_['T1'][:], AF.Exp, scale=-1.0), reads=[k('T1')], writes=[k('DM')])
                    Sd.op('dve', lambda E_: E_.scalar_tensor_tensor(B_['T1'][:], B_['GROW'][:], gcol, NEGT, op0=ALU.subtract, op1=ALU.min),
                          reads=[k('GROW'), k('sc'), 'CON', k('DM')], writes=[k('T1')])
                    Sd.op('act', lambda E_: E_.activation(B_['DTM'][:], B_['T1'][:], AF.Exp), reads=[k('T1')], writes=[k('DTM')])
                    Sd.op('dve', lambda E_: E_.tensor_tensor(sc[:, 4:5], beta, sc[:, 1:2], ALU.mult), reads=['BETA', k('sc')], writes=[k('sc')])
                    yield
                    Sd.op('pe', lambda E_: E_.matmul(pb[:, 0:128], KTh[b][:, cs], KTh[b][:, cs], start=True, stop=True), reads=[('KTh', b)], writes=[pk])
                    Sd.op('dve', lambda E_: E_.scalar_tensor_tensor(B_['M0'][:], pb[:, 0:128], beta, B_['DM'][:], op0=ALU.mult, op1=ALU.mult),
                          reads=[pk, 'BETA', k('DM')], writes=[k('M0')])
                    yield
                    Sd.op('pe', lambda E_: E_.transpose(pb[:, 0:128], B_['M0'][:], IDENT), reads=[k('M0'), 'CON'], writes=[pk])
                    Sd.op('act', lambda E_: E_.activation(B_['N0'][:], pb[:, 0:128], AF.Copy), reads=[pk], writes=[k('N0')])
                    Sd.op('dve', lambda E_: E_.tensor_tensor(B_['R0'][:], IDENT, B_['N0'][:], ALU.subtract), reads=['CON', k('N0')], writes=[k('R0')])
                    yield
                    mi = 0
                    for lev in range(1, 8):
                        if (1 << lev) >= 128 * 2:
                            break
                        Mc, Nc_, Mn, Nn = B_[f'M{mi}'], B_[f'N{mi}'], B_[f'M{1 - mi}'], B_[f'N{1 - mi}']
                        Rc, Rn = B_[f'R{mi}'], B_[f'R{1 - mi}']
                        kM, kN, kMn, kNn, kR, kRn = k(f'M{mi}'), k(f'N{mi}'), k(f'M{1 - mi}'), k(f'N{1 - mi}'), k(f'R{mi}'), k(f'R{1 - mi}')
                        lastlev = (1 << (lev + 1)) >= 256
                        Sd.op('pe', lambda E_, Mc=Mc, Nc_=Nc_: E_.matmul(pb[:, 0:128], Nc_[:], Mc[:], start=True, stop=True), reads=[kM, kN], writes=[pk])
                        Sd.op('act', lambda E_, Mn=Mn: E_.activation(Mn[:], pb[:, 0:128], AF.Copy), reads=[pk], writes=[kMn])
                        Sd.op('dve', lambda E_: E_.tensor_tensor(B_['IM'][:], pb[:, 0:128], IDENT, ALU.add), reads=[pk, 'CON'], writes=[k('IM')])
                        yield
                        if not lastlev:
                            Sd.op('pe', lambda E_, Mc=Mc, Nc_=Nc_: E_.matmul(pb[:, 0:128], Mc[:], Nc_[:], start=True, stop=True), reads=[kM, kN], writes=[pk])
                            Sd.op('act', lambda E_, Nn=Nn: E_.activation(Nn[:], pb[:, 0:128], AF.Copy), reads=[pk], writes=[kNn])
                            yield
                        Sd.op('pe', lambda E_, Rc=Rc: E_.matmul(pb[:, 0:128], B_['IM'][:], Rc[:], start=True, stop=True), reads=[k('IM'), kR], writes=[pk])
                        Sd.op('act', lambda E_, Rn=Rn: E_.activation(Rn[:], pb[:, 0:128], AF.Copy), reads=[pk], writes=[kRn])
                        yield
                        mi = 1 - mi
                    RTf = B_[f'R{mi}']
                    Sd.op('act', lambda E_: E_.activation(B_['RTh'][:], RTf[:], AF.Copy), reads=[k(f'R{mi}')], writes=[k('RTh')])
                    RT = B_['RTh']; kRT = k('RTh')
                    Sd.op('pe', lambda E_: E_.matmul(pb[:, 0:128], RT[:], VB[b][:, blk, :], start=True, stop=True), reads=[kRT, ('VB', b)], writes=[pk])
                    Sd.op('act', lambda E_: E_.activation(P_['U'][:, blk, :], pb[:, 0:128], AF.Copy), reads=[pk], writes=[('U', b)])
                    Sd.op('dve', lambda E_: E_.tensor_scalar(B_['KBG'][:], KTM[b][:, blk, :], sc[:, 4:5], None, op0=ALU.mult), reads=[('KTM', b), k('sc')], writes=[k('KBG')])
                    yield
                    Sd.op('pe', lambda E_: E_.matmul(pb[:, 0:128], B_['KBG'][:], RT[:], start=True, stop=True), reads=[kRT, k('KBG')], writes=[pk])
                    Sd.op('act', lambda E_: E_.activation(P_['WT'][:, blk, :], pb[:, 0:128], AF.Copy), reads=[pk], writes=[('WT', b)])
                    yield
                    Sd.op('pe', lambda E_: E_.matmul(pb[:, 0:128], KTh[b][:, cs], QTh[b][:, cs], start=True, stop=True), reads=[('KTh', b), ('QTh', b)], writes=[pk])
                    Sd.op('dve', lambda E_: E_.tensor_tensor(P_['ATT'][:, blk, :], pb[:, 0:128], B_['DTM'][:], ALU.mult), reads=[pk, k('DTM')], writes=[('ATT', b)])
                    Sd.op('dve', lambda E_: E_.tensor_tensor(P_['QHT'][:, blk, :], QT[b][:, cs], B_['EGR'][:], ALU.mult), reads=[('QT', b), k('EGR')], writes=[('QHT', b)])
                    Sd.op('dve', lambda E_: E_.tensor_scalar(P_['KTL'][:, blk, :], KTM[b][:, blk, :], sc[:, 2:3], None, op0=ALU.mult), reads=[('KTM', b), k('sc')], writes=[('KTL', b)])
                    yield

                def phase3(hl, tt):
                    b = tt % 2
                    P_ = P3[b]
                    for blk in range(4):
                        p0, p1 = PC[0], PC[1]
                        Sd.op('pe', lambda E_, blk=blk: E_.matmul(p0[:, 0:128], P_['WT'][:, blk, :], Sb[:], start=True, stop=True), reads=[('WT', b), 'Sb'], writes=[('PC', 0)])
                        Sd.op('dve', lambda E_, blk=blk: E_.tensor_tensor(VN[:], P_['U'][:, blk, :], p0[:, 0:128], ALU.subtract), reads=[('U', b), ('PC', 0)], writes=['VN'])
                        yield
                        Sd.op('pe', lambda E_, blk=blk: E_.matmul(p1[:, 0:128], P_['QHT'][:, blk, :], Sb[:], start=True, stop=False), reads=[('QHT', b), 'Sb'], writes=[('PC', 1)], inc=False)
                        Sd.op('pe', lambda E_, blk=blk: E_.matmul(p1[:, 0:128], P_['ATT'][:, blk, :], VN[:], start=False, stop=True), reads=[('ATT', b), 'VN'], writes=[('PC', 1)])
                        Sd.op('pe', lambda E_, blk=blk: E_.matmul(p0[:, 0:128], P_['KTL'][:, blk, :], VN[:], start=True, stop=True), reads=[('KTL', b), 'VN'], writes=[('PC', 0)])
                        Sd.op('dve', lambda E_, blk=blk: E_.scalar_tensor_tensor(Sst[:], Sst[:], P_['EGL'][:, blk:blk + 1], p0[:, 0:128], op0=ALU.mult, op1=ALU.add),
                              reads=['Sst', ('EGL', b), ('PC', 0)], writes=['Sst'])
                        Sd.op('act', lambda E_: E_.activation(Sb[:], Sst[:], AF.Copy), reads=['Sst'], writes=['Sb'])
                        Sd.op('dve', lambda E_, blk=blk: E_.memset(ss[:, blk:blk + 1], 0.0), writes=['ss'])
                        Sd.op('act', lambda E_, blk=blk: E_.activation(junk[:], p1[:, 0:128], AF.Square, accum_out=ss[:, blk:blk + 1]), reads=[('PC', 1), 'ss'], writes=['ss', 'junk'])
                        Sd.op('dve', lambda E_, blk=blk: E_.tensor_scalar(ss[:, blk:blk + 1], ss[:, blk:blk + 1], 1.0 / 128, EPS, op0=ALU.mult, op1=ALU.add), reads=['ss'], writes=['ss'])
                        Sd.op('act', lambda E_, blk=blk: E_.activation(ss[:, blk:blk + 1], ss[:, blk:blk + 1], AF.Sqrt), reads=['ss'], writes=['ss'])
                        Sd.op('dve', lambda E_, blk=blk: E_.reciprocal(ss[:, blk:blk + 1], ss[:, blk:blk + 1]), reads=['ss'], writes=['ss'])
                        Sd.op('dve', lambda E_, blk=blk: E_.scalar_tensor_tensor(Y1[:], p1[:, 0:128], ss[:, blk:blk + 1], GN[:], op0=ALU.mult, op1=ALU.mult),
                              reads=[('PC', 1), 'ss', 'GN'], writes=['Y1'])
                        Sd.op('dve', lambda E_, blk=blk: E_.tensor_tensor(Y2[:], Y1[:], ZS[tt % 3][:, blk, :], ALU.mult), reads=['Y1', ('ZS', tt % 3)], writes=['Y2'])
                        yield
                        Sd.op('pe', lambda E_, blk=blk: E_.transpose(p1[:, 0:128], Y2[:], IDENT), reads=['Y2', 'CON'], writes=[('PC', 1)])
                        Sd.op('act', lambda E_, blk=blk: E_.activation(OT[b][:, blk * 128:(blk + 1) * 128], p1[:, 0:128], AF.Copy), reads=[('PC', 1)], writes=[('OT', b)])
                        yield
                    Sd.dma('act', oloc[hl][:, tt * 512:(tt + 1) * 512], OT[b][:], reads=[('OT', b)], writes=[('oloc', hl)], stream=f'st_o{b}')

                rounds_per_head = -(-RPL // HL)
                GSTOP = getattr(cfg, 'gdn_stop', 9)
                for hl in range(HL if GSTOP > 1 else 0):
                    src = WIN[l][hl][:, :].rearrange("(t p) e -> p t e", p=128)
                    Sd.dma('sp', Wh[:], src, reads=[('WIN', l, hl)], writes=['Wh'])
                    Sd.op('dve', lambda E_: E_.memset(Sst[:], 0.0), writes=['Sst'])
                    Sd.op('dve', lambda E_: E_.memset(Sb[:], 0.0), writes=['Sb'])
                    for s_ in range(3):
                        Sd.op('dve', lambda E_, s_=s_: E_.memset(pcb[s_][:, 0:3], 0.0), writes=[('pcb', s_)])
                    for st_ in range(NTT + 2):
                        gens = []
                        if st_ < NTT:
                            gens.append(phase1(hl, st_))
                        if 1 <= st_ <= NTT and GSTOP > 2:
                            gens += [phase2(hl, st_ - 1, blk) for blk in range(4)]
                        if st_ >= 2 and GSTOP > 3:
                            gens.append(phase3(hl, st_ - 2))
                        interleave(gens)
                    drain(rounds_per_head)
                    if GSTOP > 3:
                        Sd.cc('AllGather', G4, oloc[hl][:, :], OG[hl][:, :], reads=[('oloc', hl)], writes=[('OG', hl)])
                if 'oloc0' in dbg and GSTOP > 3:
                    Sd.dma('sp', dbg['oloc0'][:, :], oloc[0][:, :], reads=[('oloc', 0)], stream='dbg3')
                Sd.barrier()
                Sd.emit()


        def phase_fox(l):
            NTT = S // 512
            with ExitStack() as ph:
                Wh = sbt(ph, 'Wh', [128, 4, E], BF16)
                Wg = sbt(ph, 'Wg', [128, KC, 2 * HL], BF16)
                Wgf = sbt(ph, 'Wgf', [128, KC, 2 * HL])
                ht = [sbt(ph, f'ht{i}', [128, KC, 512], BF16) for i in range(2)]
                GT = sbt(ph, 'GT', [128, NB, HL])
                LF = sbt(ph, 'LF', [128, HL, NB])
                HR = sbt(ph, 'HR', [128, 3 * HL]); FQK = sbt(ph, 'FQK', [128, 2]); FQS = sbt(ph, 'FQS', [128, 2])
                ONB = sbt(ph, 'ONB', [128, 128], BF16); SU = sbt(ph, 'SU', [128, 128])
                gt = [sbt(ph, f'gtmp{i}', [128, NB]) for i in range(4)]
                PA = [pst(ph, f'PA{i}', [128, 512]) for i in range(2)]
                PS = [pst(ph, f'PS{i}', [128, 512]) for i in range(2)]
                PO = [pst(ph, f'PO{i}', [128, 512]) for i in range(2)]
                PL = [pst(ph, f'PL{i}', [128, 512]) for i in range(2)]
                Sd.dma('sp', HR[:], hrow[:, :], writes=['HR'])
                Sd.dma('sp', FQK[:], fqk[:, :], writes=['FQK'])
                Sd.dma('sp', Wgf[:].rearrange("p a b -> p (a b)"), wg[:, l * KC * 2 * HL:(l + 1) * KC * 2 * HL], writes=['Wgf'])
                Sd.op('dve', lambda E_: E_.tensor_copy(Wg[:], Wgf[:]), reads=['Wgf'], writes=['Wg'])
                Sd.op('dve', lambda E_: E_.tensor_copy(ONB[:], ONES), reads=['CON'], writes=['ONB'])
                Sd.op('dve', lambda E_: E_.tensor_tensor(SU[:], UT, IDENT, ALU.subtract), reads=['CON'], writes=['SU'])
                Sd.op('dve', lambda E_: E_.tensor_scalar(FQS[:, 0:1], FQK[:, 0:1], 128.0 ** -0.5, None, op0=ALU.mult), reads=['FQK'], writes=['FQS'])
                Sd.op('dve', lambda E_: E_.tensor_copy(FQS[:, 1:2], FQK[:, 1:2]), reads=['FQK', 'FQS'], writes=['FQS'])
                for tt in range(NTT):
                    hb_ = ht[tt % 2]
                    load_h_tile(hb_, ('ht', tt % 2), tt * 512)
                    for blk in range(4):
                        for kc in range(KC):
                            Sd.op('pe', lambda E_, hb_=hb_, kc=kc, blk=blk: E_.matmul(
                                PA[blk % 2][:, 0:HL], hb_[:, kc, blk * 128:(blk + 1) * 128], Wg[:, kc, 0:HL],
                                start=(kc == 0), stop=(kc == KC - 1)), reads=[('ht', tt % 2), 'Wg'], writes=[('PA', blk % 2)], inc=(kc == KC - 1))
                        Sd.op('act', lambda E_, blk=blk, tt=tt: E_.activation(GT[:, tt * 4 + blk, :], PA[blk % 2][:, 0:HL], AF.Copy),
                              reads=[('PA', blk % 2)], writes=['GT'])
                for hl in range(HL):
                    fr = GT[:, :, hl]
                    Sd.op('dve', lambda E_, hl=hl, fr=fr: E_.tensor_scalar(gt[0][:], fr, HR[:, 2 * HL + hl:2 * HL + hl + 1], None, op0=ALU.add),
                          reads=['GT', 'HR'], writes=['gt0'])
                    Sd.op('act', lambda E_: E_.activation(gt[1][:], gt[0][:], AF.Abs), reads=['gt0'], writes=['gt1'])
                    Sd.op('act', lambda E_: E_.activation(gt[1][:], gt[1][:], AF.Exp, scale=-1.0), reads=['gt1'], writes=['gt1'])
                    Sd.op('act', lambda E_: E_.activation(gt[1][:], gt[1][:], AF.Ln, bias=1.0), reads=['gt1'], writes=['gt1'])
                    Sd.op('dve', lambda E_: E_.tensor_scalar(gt[2][:], gt[0][:], -1.0, 0.0, op0=ALU.mult, op1=ALU.max), reads=['gt0'], writes=['gt2'])
                    Sd.op('dve', lambda E_: E_.tensor_tensor(gt[2][:], gt[2][:], gt[1][:], ALU.add), reads=['gt2', 'gt1'], writes=['gt2'])
                    Sd.op('dve', lambda E_, hl=hl: E_.tensor_scalar(LF[:, hl, :], gt[2][:], -1.0, None, op0=ALU.mult), reads=['gt2'], writes=['LF'])
                KTb = sbt(ph, 'KTb', [128, S], BF16)
                VTM = sbt(ph, 'VTM', [128, NB, 128], BF16)
                CROW = sbt(ph, 'CROW', [128, S])
                CK = sbt(ph, 'CK', [128, NB]); CKN = sbt(ph, 'CKN', [128, NB]); OFF = sbt(ph, 'OFF', [128, NB]); TOTT = sbt(ph, 'TOTT', [128, 128])
                DG = [sbt(ph, f'DG{i}', [128, 128]) for i in range(2)]
                QTb = [sbt(ph, f'QTb{i}', [128, 512], BF16) for i in range(2)]
                GS = [sbt(ph, f'GS{i}', [128, 512], BF16) for i in range(2)]
                qf = sbt(ph, 'qf', [128, 512]); sqb = sbt(ph, 'sqb', [128, 512], BF16); rs = sbt(ph, 'rs', [128, 512])
                Lb = [sbt(ph, f'Lb{i}', [128, 512]) for i in range(2)]
                PT = [sbt(ph, f'PT{i}', [128, 512], BF16) for i in range(2)]
                rl = sbt(ph, 'rl', [128, 512]); otmp = sbt(ph, 'otmp', [128, 512])
                OT = [sbt(ph, f'OT{i}', [128, 512], BF16) for i in range(2)]

                def cum_head(hl):
                    lf = LF[:, hl, :]
                    Sd.op('pe', lambda E_: E_.matmul(PA[0][:, 0:NB], UT, lf, start=True, stop=True), reads=['LF', 'CON'], writes=[('PA', 0)])
                    Sd.op('pe', lambda E_: E_.matmul(PA[1][0:NB, 0:128], lf, ONES, start=True, stop=True), reads=['LF', 'CON'], writes=[('PA', 1)])
                    Sd.op('act', lambda E_: E_.activation(TOTT[0:NB, :], PA[1][0:NB, 0:128], AF.Copy), reads=[('PA', 1)], writes=['TOTT'])
                    Sd.op('act', lambda E_: E_.activation(CK[:], PA[0][:, 0:NB], AF.Copy), reads=[('PA', 0)], writes=['CK'])
                    Sd.op('pe', lambda E_: E_.matmul(PA[1][:, 0:NB], TOTT[0:NB, :], SU[0:NB, 0:NB], start=True, stop=True), reads=['TOTT', 'SU'], writes=[('PA', 1)])
                    Sd.op('dve', lambda E_: E_.tensor_tensor(CK[:], CK[:], PA[1][:, 0:NB], ALU.add), reads=['CK', ('PA', 1)], writes=['CK'])
                    Sd.op('dve', lambda E_: E_.tensor_scalar(CKN[:], CK[:], -1.0, None, op0=ALU.mult), reads=['CK'], writes=['CKN'])
                    for g4 in range(NB // 4):
                        pa = PA[g4 % 2]; pk = ('PA', g4 % 2)
                        for i4 in range(4):
                            bb = g4 * 4 + i4
                            dg = DG[bb % 2]
                            Sd.op('dve', lambda E_, dg=dg, bb=bb: E_.tensor_scalar(dg[:], IDENT, CK[:, bb:bb + 1], None, op0=ALU.mult),
                                  reads=['CON', 'CK'], writes=[('DG', bb % 2)])
                            Sd.op('pe', lambda E_, dg=dg, pa=pa, i4=i4: E_.matmul(pa[:, i4 * 128:(i4 + 1) * 128], ONES, dg[:], start=True, stop=True),
                                  reads=[('DG', bb % 2), 'CON'], writes=[pk])
                        Sd.op('act', lambda E_, pa=pa, g4=g4: E_.activation(CROW[:, g4 * 512:(g4 + 1) * 512], pa[:, :], AF.Copy), reads=[pk], writes=['CROW'])
                    yield

                def f1(hl, tt):
                    b = tt % 2
                    hb_ = ht[b]
                    load_h_tile(hb_, ('ht', b), tt * 512)
                    wv = Wh[:].rearrange("p t (k c) -> p (t k) c", c=512)
                    for s_ in range(4):
                        pa = PA[s_ % 2]; pk = ('PA', s_ % 2)
                        for kc in range(KC):
                            Sd.op('pe', lambda E_, pa=pa, kc=kc, s_=s_: E_.matmul(pa[:, :], wv[:, kc, s_ * 128:(s_ + 1) * 128], hb_[:, kc, :],
                                                                         start=(kc == 0), stop=(kc == KC - 1)),
                                  reads=['Wh', ('ht', b)], writes=[pk], inc=(kc == KC - 1))
                            if kc % 4 == 3 and kc != KC - 1:
                                yield
                        if s_ < 2:
                            Sd.op('act', lambda E_, pa=pa: E_.activation(qf[:], pa[:, :], AF.Copy), reads=[pk], writes=['qf'])
                            Sd.op('act', lambda E_: E_.activation(sqb[:], qf[:], AF.Square), reads=['qf'], writes=['sqb'])
                            Sd.op('pe', lambda E_, pa=pa: E_.matmul(pa[:, :], ONB[:], sqb[:], start=True, stop=True), reads=['sqb', 'ONB'], writes=[pk])
                            Sd.op('dve', lambda E_, pa=pa: E_.tensor_scalar(rs[:], pa[:, :], 1.0 / 128, EPS, op0=ALU.mult, op1=ALU.add), reads=[pk], writes=['rs'])
                            Sd.op('act', lambda E_: E_.activation(rs[:], rs[:], AF.Sqrt), reads=['rs'], writes=['rs'])
                            Sd.op('dve', lambda E_: E_.reciprocal(rs[:], rs[:]), reads=['rs'], writes=['rs'])
                            if s_ == 0:
                                Sd.op('dve', lambda E_: E_.scalar_tensor_tensor(QTb[b][:], qf[:], FQS[:, 0:1], rs[:], op0=ALU.mult, op1=ALU.mult),
                                      reads=['qf', 'rs', 'FQS'], writes=[('QTb', b)])
                            else:
                                Sd.op('dve', lambda E_: E_.scalar_tensor_tensor(KTb[:, tt * 512:(tt + 1) * 512], qf[:], FQS[:, 1:2], rs[:], op0=ALU.mult, op1=ALU.mult),
                                      reads=['qf', 'rs', 'FQS'], writes=[('KTb', tt)])
                        if s_ == 2:
                            Sd.op('act', lambda E_, pa=pa: E_.activation(qf[:], pa[:, :], AF.Copy), reads=[pk], writes=['qf'])
                            for blk in range(4):
                                Sd.op('pe', lambda E_, pa=pa, blk=blk: E_.transpose(pa[:, blk * 128:(blk + 1) * 128], qf[:, blk * 128:(blk + 1) * 128], IDENT),
                                      reads=['qf', 'CON'], writes=[pk], inc=(blk == 3))
                            Sd.op('act', lambda E_, pa=pa: E_.activation(VTM[:, tt * 4:tt * 4 + 4, :].rearrange("p a c -> p (a c)"), pa[:, :], AF.Copy),
                                  reads=[pk], writes=[('VTM', tt)])
                        if s_ == 3:
                            Sd.op('act', lambda E_, pa=pa: E_.activation(GS[b][:], pa[:, :], AF.Sigmoid), reads=[pk], writes=[('GS', b)])
                        yield

                cnt = [0]

                def f2(hl, qt):
                    b = qt % 2
                    po, pl = PO[b], PL[b]
                    nkb = 4 * qt + 4

                    def qk(kb):
                        c_lo = max(0, kb - 4 * qt) * 128
                        n = 512 - c_lo
                        i2 = cnt[0] % 2
                        cnt[0] += 1
                        ps, Lt, Pt = PS[i2], Lb[i2], PT[i2]
                        qcols = slice(qt * 512 + c_lo, (qt + 1) * 512)
                        Sd.op('pe', lambda E_: E_.matmul(ps[:, 0:n], KTb[:, kb * 128:(kb + 1) * 128], QTb[b][:, c_lo:512], start=True, stop=True),
                              reads=[('KTb', kb // 4), ('QTb', b)], writes=[('PS', i2)])
                        Sd.op('dve', lambda E_: E_.tensor_tensor(Lt[:, 0:n], ps[:, 0:n], CROW[:, qcols], ALU.add),
                              reads=[('PS', i2), 'CROW'], writes=[('Lb', i2)])
                        if kb >= 4 * qt:
                            Sd.op('dve', lambda E_: E_.tensor_tensor(Lt[:, 0:128], Lt[:, 0:128], NEGT, ALU.add), reads=[('Lb', i2), 'CON'], writes=[('Lb', i2)])
                        Sd.op('act', lambda E_: E_.activation(Pt[:, 0:n], Lt[:, 0:n], AF.Exp, bias=CKN[:, kb:kb + 1]),
                              reads=[('Lb', i2), 'CKN'], writes=[('PT', i2)])
                        return (kb, c_lo, n, i2, Pt)

                    def pv(kb, c_lo, n, i2, Pt):
                        Sd.op('pe', lambda E_: E_.matmul(po[:, c_lo:512], VTM[:, kb, :], Pt[:, 0:n], start=(kb == 0), stop=(kb == nkb - 1)),
                              reads=[('VTM', kb // 4), ('PT', i2)], writes=[('PO', b)], inc=False)
                        Sd.op('pe', lambda E_: E_.matmul(pl[:, c_lo:512], ONB[:], Pt[:, 0:n], start=(kb == 0), stop=(kb == nkb - 1)),
                              reads=['ONB', ('PT', i2)], writes=[('PL', b)])

                    pendq = None
                    for kb in range(nkb):
                        st = qk(kb)
                        if pendq is not None:
                            pv(*pendq)
                        pendq = st
                        yield
                    pv(*pendq)
                    Sd.op('dve', lambda E_: E_.reciprocal(rl[:], pl[:, :]), reads=[('PL', b)], writes=['rl'])
                    Sd.op('dve', lambda E_: E_.tensor_tensor(otmp[:], po[:, :], rl[:], ALU.mult), reads=[('PO', b), 'rl'], writes=['otmp'])
                    Sd.op('dve', lambda E_: E_.tensor_tensor(OT[b][:], otmp[:], GS[b][:], ALU.mult), reads=['otmp', ('GS', b)], writes=[('OT', b)])
                    Sd.dma('act', oloc[hl][:, qt * 512:(qt + 1) * 512], OT[b][:], reads=[('OT', b)], writes=[('oloc', hl)], stream=f'st_o{b}')

                rounds_per_head = -(-RPL // HL)
                for hl in range(HL):
                    src = WIN[l][hl][:, :].rearrange("(t p) e -> p t e", p=128)
                    Sd.dma('sp', Wh[:], src, reads=[('WIN', l, hl)], writes=['Wh'])
                    interleave([cum_head(hl)])
                    prev = None
                    for tt in range(NTT):
                        interleave([f1(hl, tt)] + ([f2(hl, prev)] if prev is not None else []))
                        prev = tt
                    interleave([f2(hl, prev)])
                    drain(rounds_per_head)
                    Sd.cc('AllGather', G4, oloc[hl][:, :], OG[hl][:, :], reads=[('oloc', hl)], writes=[('OG', hl)])
                if 'oloc1' in dbg:
                    Sd.dma('sp', dbg['oloc1'][:, :], oloc[0][:, :], reads=[('oloc', 0)], stream='dbg4')
                Sd.barrier()
                Sd.emit()

        RPL = TPLP // 16
        if getattr(cfg, 'stop_after', '') == 'modA':
            Sd.barrier(); Sd.emit()
            return nc
        phase_n0()
        for l in range(L):
            if getattr(cfg, 'fake_o', False) and l in getattr(cfg, 'fake_layers', (0, 1)):
                ofake = din(f'ofake{l}', [HL * 512, S], BF16)
                for hl in range(HL):
                    Sd.dma('sp', OG[hl][:, :], ofake[hl * 512:(hl + 1) * 512, :], writes=[('OG', hl)], stream=f'fk{hl}')
                drain_to((l + 1) * RPL)
            else:
                if l + 1 < L:
                    pass
                mixer = phase_gdn if l == 0 else phase_fox
                mixer(l)
                if getattr(cfg, 'stop_after', '') == ('gdn', 'fox')[l]:
                    break
                drain_to((l + 1) * RPL)
            if l + 1 < L:
                cast_win(l + 1)
                gather_win(l + 1)
            phase_tok(l)
        Sd.barrier()
        Sd.emit()
    return nc


def kernel(**inputs):
    cfg = Cfg(4096, 4096)
    maps = prep_inputs(cfg, inputs)
    nc = build(cfg)
    res = run_bass_kernel_spmd(nc, maps, core_ids=list(range(8)))
    out = np.empty((cfg.B, cfg.S, cfg.D), np.float32)
    for c in range(8):
        d, r = c // 4, c % 4
        out[d, r * cfg.TL:(r + 1) * cfg.TL, :] = np.asarray(res.results[c]['outT']).T
    return out
```

```python
import numpy as np
import ml_dtypes
from contextlib import ExitStack
import concourse.bass as bass
import concourse.mybir as mybir
from concourse.bass_utils import run_bass_kernel_spmd

F32 = mybir.dt.float32
BF16 = mybir.dt.bfloat16
AF = mybir.ActivationFunctionType
ALU = mybir.AluOpType
AX = mybir.AxisListType
NPBF = ml_dtypes.bfloat16

G4 = [[0, 1, 2, 3], [4, 5, 6, 7]]
G2 = [[0, 4], [1, 5], [2, 6], [3, 7]]
BIG = 30000.0
EPS = 1e-6


class Cfg:
    def __init__(self, D=4096, S=4096, dbg=()):
        self.D, self.S, self.B, self.L = D, S, 2, 2
        self.H = D // 128
        self.KC = D // 128
        self.F = ((8 * D // 3 + 255) // 256) * 256
        self.FC = self.F // 128
        self.TL = S // 4
        self.HL = self.H // 4
        self.NB = S // 128
        self.E = D
        self.NG0 = D // 512
        self.TPL = 4 * self.NG0 + 3 * self.FC
        self.TPLP = ((self.TPL + 15) // 16) * 16
        self.NT = self.L * self.TPLP
        self.NR = self.NT // 16
        self.NLT = self.NT // 8
        self.NCH = 6 * self.KC // 8
        self.TP = min(512, self.TL)
        self.dbg = tuple(dbg)


_PSK = {'pg', 'pm', 'ph', 'npsn0', 'PA', 'PB', 'PC', 'PS', 'PO', 'PL'}


def _is_psum(k):
    return (isinstance(k, tuple) and k[0] in _PSK) or k == 'pm'


class Sched:
    def __init__(self, nc, es):
        self.nc, self.es = nc, es
        self.eng = {'pe': nc.tensor, 'act': nc.scalar, 'dve': nc.vector, 'pool': nc.gpsimd, 'sp': nc.sync}
        self.prog = {e: [] for e in self.eng}
        self.sems, self.cnt = {}, {}
        self.waited = {e: {} for e in self.eng}
        self.lastw, self.rd = {}, {}
        self.nops = 0

    def _sem(self, stream):
        if stream not in self.sems:
            self.sems[stream] = self.es.enter_context(self.nc.semaphore('S_' + str(stream)))
            self.cnt[stream] = 0
        return self.sems[stream]

    def _emit_waits(self, eng, deps):
        for s, v in deps.items():
            if self.waited[eng].get(s, 0) >= v:
                continue
            if s == eng and (eng == 'pe' or v > self.cnt[eng]):
                continue
            self.waited[eng][s] = v
            sem = self._sem(s)
            self.prog[eng].append(lambda E, sem=sem, v=v: E.wait_ge(sem, v))

    def _waits(self, eng, reads, writes):
        deps = {}

        def add(s, v):
            if v > deps.get(s, 0):
                deps[s] = v
        for k in reads:
            if k in self.lastw:
                add(*self.lastw[k])
            if _is_psum(k):
                for (s, v) in self.rd.get(k, ()):
                    if s != eng:
                        add(s, v)
        for k in writes:
            if k in self.lastw:
                add(*self.lastw[k])
            for (s, v) in self.rd.get(k, ()):
                add(s, v)
        self._emit_waits(eng, deps)

    def _record(self, stream, val, reads, writes):
        for k in writes:
            self.lastw[k] = (stream, val)
            self.rd[k] = []
        for k in reads:
            self.rd.setdefault(k, []).append((stream, val))

    def op(self, eng, fn, reads=(), writes=(), inc=True):
        self.nops += 1
        self._waits(eng, reads, writes)
        sem = self._sem(eng)
        val = self.cnt[eng] + 1
        if inc:
            self.cnt[eng] = val
            self.prog[eng].append(lambda E, fn=fn, sem=sem: fn(E).then_inc(sem, 1))
        else:
            self.prog[eng].append(lambda E, fn=fn: fn(E))
        self._record(eng, val, reads, writes)

    def dma(self, q, out, in_, reads=(), writes=(), stream=None):
        self.nops += 1
        if stream is None:
            stream = 'd_' + str(writes[0] if writes else reads[0])
        self._waits(q, reads, writes)
        sem = self._sem(stream)
        self.cnt[stream] += 16
        val = self.cnt[stream]
        self.prog[q].append(lambda E, sem=sem, out=out, in_=in_: E.dma_start(out=out, in_=in_).then_inc(sem, 16))
        self._record(stream, val, reads, writes)

    def cc(self, kind, groups, in_, out, reads=(), writes=()):
        self._waits('pool', reads, writes)
        sem = self._sem('cc')
        self.cnt['cc'] += 1
        val = self.cnt['cc']
        op = ALU.bypass if kind == 'AllGather' else ALU.add
        self.prog['pool'].append(lambda E, sem=sem, in_=in_, out=out: E.collective_compute(
            kind, op, replica_groups=groups, ins=[in_], outs=[out]).then_inc(sem, 1))
        self.prog['pool'].append(lambda E, sem=sem, val=val: E.wait_ge(sem, val))
        self.waited['pool']['cc'] = val
        self._record('cc', val, reads, writes)

    def barrier(self):
        deps = {s: v for s, v in self.cnt.items() if v > 0}
        for e in self.eng:
            self._emit_waits(e, dict(deps))

    def emit(self):
        prog = self.prog
        with self.nc.Block() as block:
            @block.sync
            def _(E):
                for f in prog['sp']:
                    f(E)

            @block.tensor
            def _(E):
                for f in prog['pe']:
                    f(E)

            @block.scalar
            def _(E):
                for f in prog['act']:
                    f(E)

            @block.vector
            def _(E):
                for f in prog['dve']:
                    f(E)

            @block.gpsimd
            def _(E):
                for f in prog['pool']:
                    f(E)
        self.prog = {e: [] for e in self.eng}


def interleave(gens):
    gens = list(gens)
    while gens:
        alive = []
        for g in gens:
            try:
                next(g)
                alive.append(g)
            except StopIteration:
                pass
        gens = alive


def _owner(t):
    r, q, i = t % 4, (t // 4) % 4, t // 16
    d, hf = q // 2, q % 2
    return 4 * d + r, 2 * i + hf


def prep_inputs(cfg, inp):
    D, S, KC, F, FC, E, H, HL, TL, L = cfg.D, cfg.S, cfg.KC, cfg.F, cfg.FC, cfg.E, cfg.H, cfg.HL, cfg.TL, cfg.L
    MIX = D
    f32 = lambda a: np.ascontiguousarray(np.asarray(a, dtype=np.float32))
    TPLP = cfg.TPLP
    tiles = np.zeros((cfg.NT, 128, E), np.float32)
    for l in range(L):
        t = l * TPLP
        mixer_wout = inp['gdn_w_out'][0] if l == 0 else inp['fox_w_out'][0]
        wo = f32(mixer_wout).reshape(KC, 128, cfg.NG0, 512).transpose(2, 1, 0, 3)
        tiles[t:t + 4 * cfg.NG0] = wo.reshape(cfg.NG0, 128, 4, E).transpose(0, 2, 1, 3).reshape(-1, 128, E)
        t += 4 * cfg.NG0
        wu = f32(inp['ffn_w_up'][l]).reshape(KC, 128, 2, FC, 128).transpose(3, 1, 0, 2, 4)
        tiles[t:t + 2 * FC] = wu.reshape(FC, 128, 2, E).transpose(0, 2, 1, 3).reshape(-1, 128, E)
        t += 2 * FC
        tiles[t:t + FC] = f32(inp['ffn_w_down'][l]).reshape(FC, 128, D)
    own = [[None] * cfg.NLT for _ in range(8)]
    for t in range(cfg.NT):
        c, lt = _owner(t)
        own[c][lt] = t
    consts = np.zeros((128, 6, 128), np.float32)
    ii, jj = np.meshgrid(np.arange(128), np.arange(128), indexing='ij')
    consts[:, 0] = (ii == jj)
    consts[:, 1] = (ii <= jj)
    consts[:, 2] = np.where(jj < ii, 0.0, BIG)
    consts[:, 3] = np.where(jj >= ii, 0.0, -BIG)
    consts[:, 4] = 1.0
    consts[:, 5] = np.where(jj <= ii, 0.0, BIG)
    x = f32(inp['x'])
    c_in = f32(inp['c'])
    cT = np.ascontiguousarray(c_in.T.reshape(KC, 128, 2).transpose(1, 0, 2)).reshape(128, KC * 2)
    ng = np.stack([f32(inp['norm_mix_g'][0]), f32(inp['norm_ffn_g'][0]), f32(inp['norm_mix_g'][1]),
                   f32(inp['norm_ffn_g'][1]), f32(inp['final_norm_g'])], 0)
    ng = np.ascontiguousarray(ng.reshape(5, KC, 128).transpose(2, 0, 1)).reshape(128, 5 * KC)
    w_in = [f32(inp['gdn_w_in'][0]), f32(inp['fox_w_in'][0])]
    ncol = 6 * D // 8
    maps = []
    for c in range(8):
        d, r = c // 4, c % 4
        m = {}
        xt = np.zeros((D, 4 + TL), np.float32)
        lo = r * TL - 4
        if r == 0:
            xt[:, 4:] = x[d, 0:TL].T
        else:
            xt[:] = x[d, lo:lo + TL + 4].T
        m['xT'] = xt
        m['cT'] = cT
        q = 2 * r + d
        m['adaw'] = np.ascontiguousarray(f32(inp['ada_w'])[:, :, q * ncol:(q + 1) * ncol]).reshape(L * D, ncol)
        ab = f32(inp['ada_b'])[:, q * ncol:(q + 1) * ncol].reshape(L, cfg.NCH, 128)
        m['adab'] = np.ascontiguousarray(ab.transpose(2, 0, 1)).reshape(128, L * cfg.NCH)
        m['ng'] = ng
        m['consts'] = consts.reshape(128, 6 * 128)
        m['wloc'] = tiles[own[c]].reshape(cfg.NLT * 128, E)
        wl = np.zeros((L, HL, 2, 128, E), np.float32)
        wg = np.zeros((128, L, KC, 2 * HL), np.float32)
        for l in range(L):
            for hl in range(HL):
                hd = r * HL + hl
                cols = np.concatenate([np.arange(s * MIX + hd * 128, s * MIX + hd * 128 + 128) for s in range(4)])
                blk = w_in[l][:, cols].reshape(KC, 128, 512).transpose(1, 0, 2).reshape(128, 4, E)
                wl[l, hl] = blk[:, 2 * d:2 * d + 2].transpose(1, 0, 2)
            hd0 = r * HL
            if l == 0:
                gc = np.concatenate([np.arange(4 * MIX + hd0, 4 * MIX + hd0 + HL),
                                     np.arange(4 * MIX + H + hd0, 4 * MIX + H + hd0 + HL)])
            else:
                gc = np.arange(4 * MIX + hd0, 4 * MIX + hd0 + HL)
            wg[:, l, :, :len(gc)] = w_in[l][:, gc].reshape(KC, 128, len(gc)).transpose(1, 0, 2)
        m['winloc'] = wl.reshape(L * HL * 2 * 128, E)
        m['wg'] = wg.reshape(128, L * KC * 2 * HL)
        hs = slice(r * HL, (r + 1) * HL)
        cw = f32(inp['gdn_conv_w'][0]).reshape(4, 3, H, 128)[:, :, hs]
        m['gconv'] = np.ascontiguousarray(cw.transpose(3, 2, 1, 0)).reshape(128, HL * 3 * 4)
        rowrep = lambda v: np.ascontiguousarray(np.broadcast_to(f32(v)[None, :], (128, len(v))))
        m['hrow'] = np.concatenate([rowrep(inp['gdn_A_log'][0][hs]), rowrep(inp['gdn_dt_bias'][0][hs]),
                                    rowrep(inp['fox_b_f'][0][hs])], 1)
        m['gnorm'] = rowrep(inp['gdn_norm_g'][0])
        m['fqk'] = np.stack([f32(inp['fox_q_norm_g'][0]), f32(inp['fox_k_norm_g'][0])], 1)
        fc = np.stack([f32(inp['ffn_conv_w'][l]).reshape(3, 2 * FC, 128) for l in range(L)], 0)
        m['fcw'] = np.ascontiguousarray(fc.transpose(3, 0, 2, 1)).reshape(128, L * 2 * FC * 3)
        sel = np.zeros((128, 8), np.float32)
        sel[:, d] = 1.0
        sel[:, 2 + r] = 1.0
        sel[:, 6] = 0.0 if r == 0 else 1.0
        m['sel'] = sel
        maps.append(m)
    return maps


def build(cfg):
    D, S, KC, F, FC, E, H, HL, TL, L, NB = cfg.D, cfg.S, cfg.KC, cfg.F, cfg.FC, cfg.E, cfg.H, cfg.HL, cfg.TL, cfg.L, cfg.NB
    NCH, NCOL, NLT, NR, TPLP, NG0, TP = cfg.NCH, 6 * D // 8, cfg.NLT, cfg.NR, cfg.TPLP, cfg.NG0, cfg.TP
    nc = bass.Bass("TRN2", target_bir_lowering=False)
    din = lambda name, shape, dt=F32: nc.dram_tensor(name, shape, dt, kind="ExternalInput")
    xT = din('xT', [D, 4 + TL]); cT = din('cT', [128, KC * 2]); adaw = din('adaw', [L * D, NCOL])
    adab = din('adab', [128, L * NCH]); ng = din('ng', [128, 5 * KC]); consts = din('consts', [128, 768])
    wloc = din('wloc', [NLT * 128, E]); winloc = din('winloc', [L * HL * 256, E]); wg = din('wg', [128, L * KC * 2 * HL])
    gconv = din('gconv', [128, HL * 12]); hrow = din('hrow', [128, 3 * HL]); gnorm = din('gnorm', [128, 128])
    fqk = din('fqk', [128, 2]); sel = din('sel', [128, 8]); fcw = din('fcw', [128, L * 2 * FC * 3])
    outT = nc.dram_tensor('outT', [D, TL], F32, kind="ExternalOutput")
    dbg = {}
    for name, shape, dt in cfg.dbg:
        dbg[name] = nc.dram_tensor('dbg_' + name, shape, dt, kind="ExternalOutput")
    wbf = nc.dram_tensor('wbf', [NLT * 128, E], BF16)
    Pb = [nc.dram_tensor(f'Pb{i}', [512, E], BF16) for i in range(NR)]
    Gb = [nc.dram_tensor(f'Gb{i}', [2048, E], BF16) for i in range(NR)]
    winbf = nc.dram_tensor('winbf', [L * HL * 256, E], BF16)
    WIN = [[nc.dram_tensor(f'WIN{l}_{hl}', [512, E], BF16) for hl in range(HL)] for l in range(L)]
    modd = nc.dram_tensor('modd', [128, L * NCH * 2], F32)
    modP = nc.dram_tensor('modP', [256, L * NCH * 2], F32)
    modG = nc.dram_tensor('modG', [1024, L * NCH * 2], F32)

    top = ExitStack()
    with top:
        Sd = Sched(nc, top)
        uid = [0]

        def sbt(es, name, shape, dt=F32):
            uid[0] += 1
            return es.enter_context(nc.sbuf_tensor(f'{name}_{uid[0]}', shape, dt))

        def pst(es, name, shape, dt=F32):
            uid[0] += 1
            return es.enter_context(nc.psum_tensor(f'{name}_{uid[0]}', shape, dt))
        CON = sbt(top, 'CON', [128, 6, 128])
        SEL = sbt(top, 'SEL', [128, 8])
        NGs = sbt(top, 'NGs', [128, 5, KC])
        MODL = sbt(top, 'MODL', [128, L, 6 * KC])
        Sd.dma('sp', CON[:].rearrange("p a b -> p (a b)"), consts[:, :], writes=['CON'])
        Sd.dma('sp', SEL[:], sel[:, :], writes=['SEL'])
        Sd.dma('sp', NGs[:].rearrange("p a b -> p (a b)"), ng[:, :], writes=['NGs'])
        IDENT, UT, POSS, NEGT, ONES, POSI = (CON[:, i, :] for i in range(6))

        def cast_win(l):
            pass

        def gather_win(l):
            for hl in range(HL):
                u = l * HL + hl
                Sd.dma('pool', winbf[u * 256:(u + 1) * 256, :], winloc[u * 256:(u + 1) * 256, :], writes=[('winbf', u)], stream='cast')
                Sd.cc('AllGather', G2, winbf[u * 256:(u + 1) * 256, :], WIN[l][hl][:, :],
                      reads=[('winbf', u)], writes=[('WIN', l, hl)])

        def gather_round(i):
            Sd.dma('pool', wbf[i * 256:(i + 1) * 256, :], wloc[i * 256:(i + 1) * 256, :], writes=[('wbf', i)], stream='cast')
            Sd.cc('AllGather', G2, wbf[i * 256:(i + 1) * 256, :], Pb[i][:, :], reads=[('wbf', i)], writes=[('Pb', i)])
            for q in range(4):
                Sd.cc('AllGather', G4, Pb[i][q * 128:(q + 1) * 128, :], Gb[i][q * 512:(q + 1) * 512, :],
                      reads=[('Pb', i)], writes=[('Gb', i)])

        RPL = TPLP // 16
        pend = list(range(NR))

        def drain(n):
            for _ in range(min(n, len(pend))):
                gather_round(pend.pop(0))

        def drain_to(i_end):
            while pend and pend[0] < i_end:
                gather_round(pend.pop(0))

        def gtile(t, n=1):
            i, o = t // 16, t % 16
            assert o + n <= 16
            return Gb[i][o * 128:(o + n) * 128, :].rearrange("(t p) e -> p t e", p=128), ('Gb', i)

        cast_win(0)
        gather_win(0)
        with ExitStack() as ph:
            cact = sbt(ph, 'cact', [128, KC, 2])
            wA = [sbt(ph, f'wA{i}', [128, NCOL]) for i in range(4)]
            adb = sbt(ph, 'adb', [128, L * NCH])
            modloc = sbt(ph, 'modloc', [128, L * NCH, 2])
            MODg = sbt(ph, 'MODg', [128, 8, L * NCH, 2])
            pm = pst(ph, 'pm', [128, L * NCH, 2])
            Sd.dma('sp', cact[:].rearrange("p a b -> p (a b)"), cT[:, :], writes=['cact'])
            Sd.dma('sp', adb[:], adab[:, :], writes=['adb'])
            Sd.op('act', lambda E_: E_.activation(cact[:], cact[:], AF.Silu), reads=['cact'], writes=['cact'])
            groups = [(c0, min(512, NCOL - c0)) for c0 in range(0, NCOL, 512)]
            pgs = [pst(ph, f'pg{i}', [128, 512]) for i in range(len(groups))]
            modrow = sbt(ph, 'modrow', [2, NCOL])
            for l in range(L):
                for kc in range(KC):
                    b = (l * KC + kc) % 4
                    hcol = (NCOL // 1024) * 512 if NCOL >= 1024 else NCOL // 2
                    Sd.dma('sp', wA[b][:, 0:hcol], adaw[l * D + kc * 128:l * D + (kc + 1) * 128, 0:hcol], writes=[('wA', b, 0)])
                    Sd.dma('act', wA[b][:, hcol:NCOL], adaw[l * D + kc * 128:l * D + (kc + 1) * 128, hcol:NCOL], writes=[('wA', b, 1)])
                    for gi, (c0, n) in enumerate(groups):
                        Sd.op('pe', lambda E_, b=b, gi=gi, c0=c0, n=n, kc=kc: E_.matmul(
                            pgs[gi][0:2, 0:n], cact[:, kc, :], wA[b][:, c0:c0 + n], start=(kc == 0), stop=(kc == KC - 1)),
                            reads=[('wA', b, 0), ('wA', b, 1), 'cact'], writes=[('pg', gi)], inc=(gi == len(groups) - 1))
                for gi, (c0, n) in enumerate(groups):
                    Sd.op('act', lambda E_, gi=gi, c0=c0, n=n: E_.activation(modrow[0:2, c0:c0 + n], pgs[gi][0:2, 0:n], AF.Copy),
                          reads=[('pg', gi)], writes=['modrow'])
                for n_ in range(NCH):
                    Sd.op('pe', lambda E_, n_=n_, l=l: E_.transpose(pm[:, l * NCH + n_, :], modrow[0:2, n_ * 128:(n_ + 1) * 128], IDENT[0:2, 0:2]),
                          reads=['modrow', 'CON'], writes=['pm'], inc=(n_ == NCH - 1))
            for b in range(2):
                Sd.op('dve', lambda E_, b=b: E_.tensor_tensor(modloc[:, :, b], pm[:, :, b], adb[:], ALU.add),
                      reads=['pm', 'adb'], writes=['modloc'])
            Sd.dma('sp', modd[:, :], modloc[:].rearrange("p a b -> p (a b)"), reads=['modloc'], writes=['modd'])
            Sd.cc('AllGather', G2, modd[:, :], modP[:, :], reads=['modd'], writes=['modP'])
            Sd.cc('AllGather', G4, modP[:, :], modG[:, :], reads=['modP'], writes=['modG'])
            Sd.dma('sp', MODg[:].rearrange("p q a b -> p q (a b)"), modG[:, :].rearrange("(q p) f -> p q f", p=128),
                   reads=['modG'], writes=['MODg'])
            for l in range(L):
                ov = MODL[:, l, :].rearrange("p (q n) -> p q n", q=8)
                Sd.op('dve', lambda E_, l=l, ov=ov: E_.tensor_scalar(
                    ov, MODg[:, :, l * NCH:(l + 1) * NCH, 0], SEL[:, 0:1], None, op0=ALU.mult),
                    reads=['MODg', 'SEL'], writes=['MODL'])
                Sd.op('dve', lambda E_, l=l, ov=ov: E_.scalar_tensor_tensor(
                    ov, MODg[:, :, l * NCH:(l + 1) * NCH, 1], SEL[:, 1:2], ov, op0=ALU.mult, op1=ALU.add),
                    reads=['MODg', 'SEL', 'MODL'], writes=['MODL'])
            if 'modl' in dbg:
                Sd.dma('sp', dbg['modl'][:, :], MODL[:].rearrange("p a b -> p (a b)"), reads=['MODL'], stream='dbg')
            Sd.barrier()
            Sd.emit()

        DER = sbt(top, 'DER', [128, L, 2, KC])
        FCW = sbt(top, 'FCW', [128, L, 2 * FC, 3])
        Sd.dma('sp', FCW[:].rearrange("p a b c -> p (a b c)"), fcw[:, :], writes=['FCW'])
        for l in range(L):
            for j, (part, gi) in enumerate(((1, 2 * l), (4, 2 * l + 1))):
                Sd.op('dve', lambda E_, l=l, j=j, part=part, gi=gi: E_.scalar_tensor_tensor(
                    DER[:, l, j, :], MODL[:, l, part * KC:(part + 1) * KC], 1.0, NGs[:, gi, :], op0=ALU.add, op1=ALU.mult),
                    reads=['MODL', 'NGs'], writes=['DER'])
        modv = lambda l, part: MODL[:, l, part * KC:(part + 1) * KC]

        NHC = max(1, (D * TL * 2) // (1 << 20))
        RW = D // NHC
        hs = nc.dram_tensor('hs', [D, TL], BF16)
        HG = [nc.dram_tensor(f'HG{j}', [4 * RW, TL], BF16) for j in range(NHC)]
        xbuf1 = nc.dram_tensor('xbuf1', [D, 4 + TL], F32)
        oloc = [nc.dram_tensor(f'oloc{hl}', [128, S], BF16) for hl in range(HL)]
        OG = [nc.dram_tensor(f'OG{hl}', [512, S], BF16) for hl in range(HL)]

        def gather_h(gen):
            for j in range(NHC):
                Sd.cc('AllGather', G4, hs[j * RW:(j + 1) * RW, :], HG[j][:, :], reads=['hs'], writes=[('HG', j)])

        def norm_scratch(ph, tag, nps, npk, tmp=None, rstd=None):
            ns = {'tag': tag, 'nps': nps, 'npk': npk}
            ns['sq'] = [sbt(ph, f'sq{tag}{i}', [128, 516], BF16) for i in range(2)]
            ns['tmp'] = tmp if tmp is not None else [sbt(ph, f'tmpn{tag}{i}', [128, 516]) for i in range(2)]
            ns['tmpk'] = [('Ub', 0), ('Ub', 1)] if tmp is not None else [('tmpn', tag, 0), ('tmpn', tag, 1)]
            ns['rstd'] = rstd[0] if rstd is not None else sbt(ph, f'rstd{tag}', [128, 516])
            ns['rk'] = rstd[1] if rstd is not None else ('rstd', tag)
            ns['ones'] = sbt(ph, f'onesb{tag}', [128, 128], BF16)
            Sd.op('dve', lambda E_: E_.tensor_copy(ns['ones'][:], ONES), reads=['CON'], writes=[('onesb', tag)])
            return ns

        def norm_to(ns, xm, c0, w, Avec, Bvec, out_fn):
            tag, sq, tmp, rstd, ones_b, nps = ns['tag'], ns['sq'], ns['tmp'], ns['rstd'], ns['ones'], ns['nps']
            segs = [(0, w)] if w <= 512 else [(w - 512, 512), (0, w - 512)]
            for si, (a, n) in enumerate(segs):
                for kc in range(KC):
                    b = kc % 2
                    Sd.op('act', lambda E_, kc=kc, b=b, a=a, n=n: E_.activation(sq[b][:, 0:n], xm[:, kc, c0 + a:c0 + a + n], AF.Square),
                          reads=[('xm', kc)], writes=[('sq', tag, b)])
                    Sd.op('pe', lambda E_, kc=kc, b=b, si=si, n=n: E_.matmul(nps[si][:, 0:n], ones_b[:], sq[b][:, 0:n],
                                                                     start=(kc == 0), stop=(kc == KC - 1)),
                          reads=[('sq', tag, b), ('onesb', tag)], writes=[ns['npk'][si]])
                Sd.op('dve', lambda E_, si=si, a=a, n=n: E_.tensor_scalar(rstd[:, a:a + n], nps[si][:, 0:n], 1.0 / D, EPS, op0=ALU.mult, op1=ALU.add),
                      reads=[ns['npk'][si]], writes=[ns['rk']])
            Sd.op('act', lambda E_: E_.activation(rstd[:, 0:w], rstd[:, 0:w], AF.Sqrt), reads=[ns['rk']], writes=[ns['rk']])
            Sd.op('dve', lambda E_: E_.reciprocal(rstd[:, 0:w], rstd[:, 0:w]), reads=[ns['rk']], writes=[ns['rk']])
            for kc in range(KC):
                b = kc % 2
                oap, okeys = out_fn(kc)
                Sd.op('dve', lambda E_, kc=kc, b=b: E_.tensor_tensor(tmp[b][:, 0:w], xm[:, kc, c0:c0 + w], rstd[:, 0:w], ALU.mult),
                      reads=[('xm', kc), ns['rk']], writes=[ns['tmpk'][b]])
                if Bvec is not None:
                    Sd.op('act', lambda E_, kc=kc, b=b, oap=oap: E_.activation(oap, tmp[b][:, 0:w], AF.Identity,
                                                                        bias=Bvec[:, kc:kc + 1], scale=Avec[:, kc:kc + 1]),
                          reads=[ns['tmpk'][b], 'MODL', 'DER', 'NGs'], writes=okeys)
                else:
                    Sd.op('act', lambda E_, kc=kc, b=b, oap=oap: E_.activation(oap, tmp[b][:, 0:w], AF.Copy, scale=Avec[:, kc:kc + 1]),
                          reads=[ns['tmpk'][b], 'NGs'], writes=okeys)

        def phase_n0():
            with ExitStack() as ph:
                xm = sbt(ph, 'xm', [128, KC, 516])
                hb = sbt(ph, 'hb', [128, KC, 516], BF16)
                ns = norm_scratch(ph, 'n0', [pst(ph, 'npsn0', [128, 512])], [('npsn0',)])
                for p in range(TL // TP):
                    t0 = p * TP
                    Sd.dma('sp', xm[:, :, 0:TP], xT[:, 4 + t0:4 + t0 + TP].rearrange("(k p) t -> p k t", p=128), writes=[('xm', kc) for kc in range(KC)], stream='ld_xm')
                    norm_to(ns, xm, 0, TP, DER[:, 0, 0, :], modv(0, 0), lambda kc: (hb[:, kc, 0:TP], [('hb', kc)]))
                    Sd.dma('act', hs[:, t0:t0 + TP].rearrange("(k p) t -> p k t", p=128), hb[:, :, 0:TP], reads=[('hb', kc) for kc in range(KC)], writes=['hs'], stream='st_hs')
                gather_h(0)
                Sd.barrier()
                Sd.emit()

        def phase_tok(l):
            last = (l == L - 1)
            ho = 0 if last else 2
            hu = ho + 2
            W = hu + TP
            xin = xT if l == 0 else xbuf1
            tbase = l * TPLP
            GD = 4
            if not last:
                drain(RPL)
            with ExitStack() as ph:
                xm = sbt(ph, 'xm', [128, KC, 516])
                hb = sbt(ph, 'hb', [128, KC, 516], BF16)
                cand = [sbt(ph, 'cand0', [128, 4, 4, 516], BF16)] * 2
                WB = sbt(ph, 'WB', [128, 4, 2 * E], BF16)
                Ub = [sbt(ph, f'Ub{i}', [128, 516]) for i in range(2)]
                cg = sbt(ph, 'cg', [128, 516]); cv = sbt(ph, 'cv', [128, 516])
                actb = [sbt(ph, f'actb{i}', [128, GD, 516], BF16) for i in range(2)]
                Usave = sbt(ph, 'Usave', [128, 2 * FC, 4])
                NPASS = TL // TP
                outf = Ub
                pmain = [pst(ph, f'pmain{i}', [128, 512]) for i in range(4)]
                phal = [pst(ph, f'phal{i}', [128, 512]) for i in range(4)]
                ns = norm_scratch(ph, f't{l}', [pmain[0], phal[0]], [('pm', 0), ('ph', 0)], tmp=Ub, rstd=(cg, 'cg'))
                nslot = [0]

                def wslot():
                    nslot[0] += 1
                    return nslot[0] % 4

                for p in range(TL // TP):
                    t0 = p * TP
                    Sd.dma('sp', xm[:, :, 0:W], xin[:, 4 + t0 - hu:4 + t0 + TP].rearrange("(k p) t -> p k t", p=128), writes=[('xm', kc) for kc in range(KC)], stream='ld_xm')
                    for hl in range(HL):
                        cb = cand[hl % 2]
                        for j in range(4):
                            c_lo = j * TL + t0 - hu
                            src = OG[hl][:, :].rearrange("(a p) t -> p a t", p=128)
                            if c_lo < 0:
                                Sd.dma('sp', cb[:, j, :, hu:W], src[:, :, 0:TP], reads=[('OG', hl)], writes=[('cand', 0)])
                                Sd.dma('sp', cb[:, j, :, 0:hu], src[:, :, 0:hu], reads=[('OG', hl)], writes=[('cand', 0)])
                            else:
                                Sd.dma('sp', cb[:, j, :, 0:W], src[:, :, c_lo:c_lo + W], reads=[('OG', hl)], writes=[('cand', 0)])
                        ov = hb[:].rearrange("p (a h) w -> p a h w", h=HL)[:, :, hl, 0:W]
                        okeys = [('hb', a * HL + hl) for a in range(4)]
                        Sd.op('dve', lambda E_, cb=cb, ov=ov: E_.tensor_scalar(ov, cb[:, 0, :, 0:W], SEL[:, 2:3], None, op0=ALU.mult),
                              reads=[('cand', 0), 'SEL'], writes=okeys)
                        for j in range(1, 4):
                            Sd.op('dve', lambda E_, cb=cb, ov=ov, j=j: E_.scalar_tensor_tensor(
                                ov, cb[:, j, :, 0:W], SEL[:, 2 + j:3 + j], ov, op0=ALU.mult, op1=ALU.add),
                                reads=[('cand', 0), 'SEL'] + okeys, writes=okeys)
                    g1 = modv(l, 2)
                    for g in range(NG0):
                        sl = wslot()
                        src, gk = gtile(tbase + 4 * g, 4)
                        sl2 = wslot()
                        Sd.dma('sp', WB[:, sl, :].rearrange("p (t e) -> p t e", t=2), src[:, 0:2, :], reads=[gk], writes=[('WB', sl)])
                        Sd.dma('sp', WB[:, sl2, :].rearrange("p (t e) -> p t e", t=2), src[:, 2:4, :], reads=[gk], writes=[('WB', sl2)])
                        for n4 in range(4):
                            n = g * 4 + n4
                            pi = 2 + n % 2
                            for (kind, c_a, c_n) in ((('m', hu, TP), ('h', 0, hu)) if p == 0 else (('m', hu, TP),)):
                                for kc in range(KC):
                                    wsl = sl if kc < KC // 2 else sl2
                                    wv = WB[:, wsl, :].rearrange("p (k j) -> p k j", j=512)[:, kc % (KC // 2), n4 * 128:(n4 + 1) * 128]
                                    out_ap = pmain[pi][:, 0:TP] if kind == 'm' else phal[pi][:, 0:hu]
                                    okey = ('pm', pi) if kind == 'm' else ('ph', pi)
                                    Sd.op('pe', lambda E_, wv=wv, out_ap=out_ap, kc=kc, c_a=c_a, c_n=c_n: E_.matmul(
                                        out_ap, wv, hb[:, kc, c_a:c_a + c_n], start=(kc == 0), stop=(kc == KC - 1)),
                                        reads=[('WB', wsl), ('hb', kc)], writes=[okey], inc=(kc == KC - 1))
                                Sd.op('dve', lambda E_, out_ap=out_ap, n=n, c_a=c_a, c_n=c_n: E_.scalar_tensor_tensor(
                                    xm[:, n, c_a:c_a + c_n], out_ap, g1[:, n:n + 1], xm[:, n, c_a:c_a + c_n], op0=ALU.mult, op1=ALU.add),
                                    reads=[okey, 'MODL', ('xm', n)], writes=[('xm', n)])
                    norm_to(ns, xm, 0, W, DER[:, l, 1, :], modv(l, 3), lambda kc: (hb[:, kc, 0:W], [('hb', kc)]))
                    if p == 0:
                        Sd.op('dve', lambda E_: E_.tensor_scalar(hb[:, :, 0:hu], hb[:, :, 0:hu], SEL[:, 6:7], None, op0=ALU.mult),
                              reads=[('hb', kc) for kc in range(KC)] + ['SEL'], writes=[('hb', kc) for kc in range(KC)])
                    g2 = modv(l, 5)
                    WO = TP + ho
                    xo = hu - ho

                    def down_group(grp, ab):
                        fcs = list(range(grp * GD, min(FC, (grp + 1) * GD)))
                        slots = []
                        for h2 in range(0, len(fcs), 2):
                            sub = fcs[h2:h2 + 2]
                            sl = wslot()
                            src, gk = gtile(tbase + 4 * NG0 + 2 * FC + sub[0], len(sub))
                            Sd.dma('sp', WB[:, sl, 0:len(sub) * E].rearrange("p (t e) -> p t e", e=E), src, reads=[gk], writes=[('WB', sl)])
                            slots.append(sl)
                        for dc in range(KC):
                            pi = 2 + dc % 2
                            for (kind, c_a, c_n) in (('m', ho, TP), ('h', 0, ho)):
                                if c_n == 0 or (kind == 'h' and p > 0):
                                    continue
                                out_ap = pmain[pi][:, 0:TP] if kind == 'm' else phal[pi][:, 0:ho]
                                okey = ('pm', pi) if kind == 'm' else ('ph', pi)
                                for i, fc in enumerate(fcs):
                                    sl = slots[i // 2]
                                    Sd.op('pe', lambda E_, out_ap=out_ap, i=i, dc=dc, c_a=c_a, c_n=c_n, sl=sl: E_.matmul(
                                        out_ap, WB[:, sl, (i % 2) * E + dc * 128:(i % 2) * E + (dc + 1) * 128], ab[:, i, c_a:c_a + c_n],
                                        start=(i == 0), stop=(i == len(fcs) - 1)),
                                        reads=[('WB', sl), ('actb', id(ab))], writes=[okey], inc=(i == len(fcs) - 1))
                                Sd.op('dve', lambda E_, out_ap=out_ap, dc=dc, c_a=c_a, c_n=c_n: E_.scalar_tensor_tensor(
                                    xm[:, dc, xo + c_a:xo + c_a + c_n], out_ap, g2[:, dc:dc + 1], xm[:, dc, xo + c_a:xo + c_a + c_n],
                                    op0=ALU.mult, op1=ALU.add), reads=[okey, 'MODL', ('xm', dc)], writes=[('xm', dc)])

                    pending = None
                    for j in range(FC):
                        grp, gi = j // GD, j % GD
                        ab = actb[grp % 2]
                        sl = wslot()
                        src, gk = gtile(tbase + 4 * NG0 + 2 * j, 2)
                        Sd.dma('sp', WB[:, sl, :].rearrange("p (t e) -> p t e", t=2), src, reads=[gk], writes=[('WB', sl)])
                        wv = WB[:, sl, :].rearrange("p (k c) -> p k c", c=256)
                        for gv in range(2):
                            ub = Ub[gv]
                            ch = gv * FC + j
                            for (kind, c_a, c_n) in ((('m', hu, TP), ('h', 0, hu)) if p == 0 else (('m', hu, TP),)):
                                out_ap = pmain[gv][:, 0:TP] if kind == 'm' else phal[gv][:, 0:hu]
                                okey = ('pm', gv) if kind == 'm' else ('ph', gv)
                                for kc in range(KC):
                                    Sd.op('pe', lambda E_, out_ap=out_ap, kc=kc, gv=gv, c_a=c_a, c_n=c_n, wv=wv: E_.matmul(
                                        out_ap, wv[:, kc, gv * 128:(gv + 1) * 128], hb[:, kc, c_a:c_a + c_n],
                                        start=(kc == 0), stop=(kc == KC - 1)),
                                        reads=[('WB', sl), ('hb', kc)], writes=[okey], inc=(kc == KC - 1))
                                Sd.op('act', lambda E_, out_ap=out_ap, ub=ub, c_a=c_a, c_n=c_n: E_.activation(ub[:, c_a:c_a + c_n], out_ap, AF.Copy),
                                      reads=[okey], writes=[('Ub', gv)])
                            if p > 0:
                                Sd.op('act', lambda E_, ub=ub, ch=ch: E_.activation(ub[:, 0:hu], Usave[:, ch, 4 - hu:4], AF.Copy),
                                      reads=[('Usave', ch)], writes=[('Ub', gv)])
                            if p < NPASS - 1:
                                Sd.op('act', lambda E_, ub=ub, ch=ch: E_.activation(Usave[:, ch, 4 - hu:4], ub[:, W - hu:W], AF.Copy),
                                      reads=[('Ub', gv)], writes=[('Usave', ch)])
                            dst = cg if gv == 0 else cv
                            dk = 'cg' if gv == 0 else 'cv'
                            Sd.op('dve', lambda E_, ub=ub, dst=dst, ch=ch: E_.tensor_scalar(
                                dst[:, 0:WO], ub[:, xo:xo + WO], FCW[:, l, ch, 2:3], None, op0=ALU.mult),
                                reads=[('Ub', gv), 'FCW'], writes=[dk])
                            for tap in (1, 0):
                                sh = 2 - tap
                                Sd.op('dve', lambda E_, ub=ub, dst=dst, ch=ch, tap=tap, sh=sh: E_.scalar_tensor_tensor(
                                    dst[:, 0:WO], ub[:, xo - sh:xo - sh + WO], FCW[:, l, ch, tap:tap + 1], dst[:, 0:WO],
                                    op0=ALU.mult, op1=ALU.add), reads=[('Ub', gv), 'FCW', dk], writes=[dk])
                        Sd.op('act', lambda E_: E_.activation(cg[:, 0:WO], cg[:, 0:WO], AF.Silu), reads=['cg'], writes=['cg'])
                        Sd.op('dve', lambda E_, ab=ab, gi=gi: E_.tensor_tensor(ab[:, gi, 0:WO], cg[:, 0:WO], cv[:, 0:WO], ALU.mult),
                              reads=['cg', 'cv'], writes=[('actb', id(ab))])
                        if gi == GD - 1 or j == FC - 1:
                            if pending is not None:
                                down_group(*pending)
                            pending = (grp, ab)
                    down_group(*pending)
                    if not last:
                        so = ho if p == 0 else 0
                        Sd.dma('act', xbuf1[:, 4 + t0 - so:4 + t0 + TP].rearrange("(k p) t -> p k t", p=128), xm[:, :, hu - so:hu + TP],
                               reads=[('xm', kc) for kc in range(KC)], writes=['xbuf1'], stream='st_x')
                        norm_to(ns, xm, hu, TP, DER[:, l + 1, 0, :], modv(l + 1, 0), lambda kc: (hb[:, kc, 0:TP], [('hb', kc)]))
                        Sd.dma('act', hs[:, t0:t0 + TP].rearrange("(k p) t -> p k t", p=128), hb[:, :, 0:TP], reads=[('hb', kc) for kc in range(KC)], writes=['hs'], stream='st_hs')
                    else:
                        def ofn(kc):
                            return outf[kc % 2][:, 0:TP], [('outf', kc % 2)]
                        tag = ns['tag']
                        norm_final(ns, xm, hu, TP, NGs[:, 4, :], outf, t0)
                if not last:
                    gather_h(l + 1)
                Sd.barrier()
                Sd.emit()

        def norm_final(ns, xm, c0, w, Avec, outf, t0):
            tag, sq, tmp, rstd, ones_b, nps = ns['tag'], ns['sq'], ns['tmp'], ns['rstd'], ns['ones'], ns['nps']
            for kc in range(KC):
                b = kc % 2
                Sd.op('act', lambda E_, kc=kc, b=b: E_.activation(sq[b][:, 0:w], xm[:, kc, c0:c0 + w], AF.Square),
                      reads=[('xm', kc)], writes=[('sq', tag, b)])
                Sd.op('pe', lambda E_, kc=kc, b=b: E_.matmul(nps[0][:, 0:w], ones_b[:], sq[b][:, 0:w], start=(kc == 0), stop=(kc == KC - 1)),
                      reads=[('sq', tag, b), ('onesb', tag)], writes=[ns['npk'][0]])
            Sd.op('dve', lambda E_: E_.tensor_scalar(rstd[:, 0:w], nps[0][:, 0:w], 1.0 / D, EPS, op0=ALU.mult, op1=ALU.add),
                  reads=[ns['npk'][0]], writes=[ns['rk']])
            Sd.op('act', lambda E_: E_.activation(rstd[:, 0:w], rstd[:, 0:w], AF.Sqrt), reads=[ns['rk']], writes=[ns['rk']])
            Sd.op('dve', lambda E_: E_.reciprocal(rstd[:, 0:w], rstd[:, 0:w]), reads=[ns['rk']], writes=[ns['rk']])
            for kc in range(KC):
                b = kc % 2
                Sd.op('dve', lambda E_, kc=kc, b=b: E_.scalar_tensor_tensor(outf[b][:, 0:w], xm[:, kc, c0:c0 + w], Avec[:, kc:kc + 1], rstd[:, 0:w],
                                                                    op0=ALU.mult, op1=ALU.mult),
                      reads=[('xm', kc), ns['rk'], 'NGs'], writes=[('Ub', b)])
                Sd.dma('act', outT[kc * 128:(kc + 1) * 128, t0:t0 + w], outf[b][:, 0:w], reads=[('Ub', b)], stream=f'st_out{b}')


        def load_h_tile(ht, key, c0):
            rank, col = c0 // TL, c0 % TL
            for j in range(NHC):
                a = RW // 128
                Sd.dma('sp', ht[:, j * a:(j + 1) * a, :], HG[j][rank * RW:(rank + 1) * RW, col:col + 512].rearrange("(a p) t -> p a t", p=128),
                       reads=[('HG', j)], writes=[key])

        def phase_gdn(l):
            NTT = S // 512
            with ExitStack() as ph:
                Wh = sbt(ph, 'Wh', [128, 4, E], BF16)
                Wg = sbt(ph, 'Wg', [128, KC, 2 * HL], BF16)
                Wgf = sbt(ph, 'Wgf', [128, KC, 2 * HL])
                ht = [sbt(ph, f'ht{i}', [128, KC, 512], BF16) for i in range(2)]
                GT = sbt(ph, 'GT', [128, NB, 2 * HL])
                BETA = sbt(ph, 'BETA', [128, HL, NB]); LA = sbt(ph, 'LA', [128, HL, NB])
                HR = sbt(ph, 'HR', [128, 3 * HL]); GCV = sbt(ph, 'GCV', [128, HL, 3, 4]); GN = sbt(ph, 'GN', [128, 128])
                IDB = sbt(ph, 'IDB', [128, 128], BF16); ONB = sbt(ph, 'ONB', [128, 128], BF16)
                gt = [sbt(ph, f'gtmp{i}', [128, NB]) for i in range(4)]
                nexpA = sbt(ph, 'nexpA', [128, HL])
                PA = [pst(ph, f'PA{i}', [128, 512]) for i in range(2)]
                PBk = [pst(ph, f'PB{i}', [128, 512]) for i in range(4)]
                PC = [pst(ph, f'PC{i}', [128, 512]) for i in range(2)]
                Sd.dma('sp', HR[:], hrow[:, :], writes=['HR'])
                Sd.dma('sp', GCV[:].rearrange("p a b c -> p (a b c)"), gconv[:, :], writes=['GCV'])
                Sd.dma('sp', GN[:], gnorm[:, :], writes=['GN'])
                Sd.dma('sp', Wgf[:].rearrange("p a b -> p (a b)"), wg[:, l * KC * 2 * HL:(l + 1) * KC * 2 * HL], writes=['Wgf'])
                Sd.op('dve', lambda E_: E_.tensor_copy(Wg[:], Wgf[:]), reads=['Wgf'], writes=['Wg'])
                Sd.op('dve', lambda E_: E_.tensor_copy(IDB[:], IDENT), reads=['CON'], writes=['IDB'])
                Sd.op('dve', lambda E_: E_.tensor_copy(ONB[:], ONES), reads=['CON'], writes=['ONB'])
                for tt in range(NTT):
                    hb_ = ht[tt % 2]
                    load_h_tile(hb_, ('ht', tt % 2), tt * 512)
                    for blk in range(4):
                        for kc in range(KC):
                            Sd.op('pe', lambda E_, hb_=hb_, kc=kc, blk=blk: E_.matmul(
                                PA[blk % 2][:, 0:2 * HL], hb_[:, kc, blk * 128:(blk + 1) * 128], Wg[:, kc, :],
                                start=(kc == 0), stop=(kc == KC - 1)), reads=[('ht', tt % 2), 'Wg'], writes=[('PA', blk % 2)], inc=(kc == KC - 1))
                        Sd.op('act', lambda E_, blk=blk, tt=tt: E_.activation(GT[:, tt * 4 + blk, :], PA[blk % 2][:, 0:2 * HL], AF.Copy),
                              reads=[('PA', blk % 2)], writes=['GT'])
                Sd.op('act', lambda E_: E_.activation(nexpA[:], HR[:, 0:HL], AF.Exp), reads=['HR'], writes=['nexpA'])
                Sd.op('dve', lambda E_: E_.tensor_scalar(nexpA[:], nexpA[:], -1.0, None, op0=ALU.mult), reads=['nexpA'], writes=['nexpA'])
                for hl in range(HL):
                    braw = GT[:, :, hl]; araw = GT[:, :, HL + hl]
                    Sd.op('act', lambda E_, hl=hl, braw=braw: E_.activation(BETA[:, hl, :], braw, AF.Sigmoid), reads=['GT'], writes=['BETA'])
                    Sd.op('dve', lambda E_, hl=hl, araw=araw: E_.tensor_scalar(gt[0][:], araw, HR[:, HL + hl:HL + hl + 1], None, op0=ALU.add),
                          reads=['GT', 'HR'], writes=['gt0'])
                    Sd.op('act', lambda E_: E_.activation(gt[1][:], gt[0][:], AF.Abs), reads=['gt0'], writes=['gt1'])
                    Sd.op('act', lambda E_: E_.activation(gt[1][:], gt[1][:], AF.Exp, scale=-1.0), reads=['gt1'], writes=['gt1'])
                    Sd.op('act', lambda E_: E_.activation(gt[1][:], gt[1][:], AF.Ln, bias=1.0), reads=['gt1'], writes=['gt1'])
                    Sd.op('dve', lambda E_: E_.scalar_tensor_tensor(gt[2][:], gt[0][:], 0.0, gt[1][:], op0=ALU.max, op1=ALU.add),
                          reads=['gt0', 'gt1'], writes=['gt2'])
                    Sd.op('dve', lambda E_, hl=hl: E_.tensor_scalar(LA[:, hl, :], gt[2][:], nexpA[:, hl:hl + 1], None, op0=ALU.mult),
                          reads=['gt2', 'nexpA'], writes=['LA'])
                if 'gates' in dbg:
                    Sd.dma('sp', dbg['gates'][:, 0:HL * NB], BETA[:].rearrange("p a b -> p (a b)"), reads=['BETA'], stream='dbg')
                    Sd.dma('sp', dbg['gates'][:, HL * NB:2 * HL * NB], LA[:].rearrange("p a b -> p (a b)"), reads=['LA'], stream='dbg2')
                pcb = [sbt(ph, f'pcb{s_}', [128, 515]) for s_ in range(3)]
                post = [sbt(ph, f'post{s_}', [128, 512]) for s_ in range(3)]
                sqb = sbt(ph, 'sqb', [128, 512], BF16); rs = sbt(ph, 'rs', [128, 512])
                zsf = sbt(ph, 'zsf', [128, 512])
                QT = [sbt(ph, f'QT{i}', [128, 512]) for i in range(2)]
                KT = [sbt(ph, f'KT{i}', [128, 512]) for i in range(2)]
                KTM = [sbt(ph, f'KTM{i}', [128, 4, 128]) for i in range(2)]
                VB = [sbt(ph, f'VB{i}', [128, 4, 128]) for i in range(2)]
                ZS = [sbt(ph, f'ZS{i}', [128, 4, 128], BF16) for i in range(3)]
                def blkbufs(i):
                    d_ = {}
                    for nm in ('LAB', 'GROW', 'EGR', 'T1', 'DM', 'DTM', 'M0', 'M1', 'N0', 'N1', 'IM', 'R0', 'R1', 'KBG'):
                        d_[nm] = sbt(ph, f'{nm}_{i}', [128, 128])
                    d_['sc'] = sbt(ph, f'sc_{i}', [128, 8])
                    return d_
                BB = [blkbufs(i) for i in range(4)]
                def p3bufs(i):
                    d_ = {}
                    for nm in ('U', 'WT', 'QHT', 'ATT', 'KTL'):
                        d_[nm] = sbt(ph, f'{nm}_{i}', [128, 4, 128])
                    d_['EGL'] = sbt(ph, f'EGL_{i}', [128, 4])
                    return d_
                P3 = [p3bufs(i) for i in range(2)]
                Sst = sbt(ph, 'Sst', [128, 128]); VN = sbt(ph, 'VN', [128, 128])
                ss = sbt(ph, 'ss', [128, 4]); junk = sbt(ph, 'junk', [128, 128])
                Y1 = sbt(ph, 'Y1', [128, 128]); Y2 = sbt(ph, 'Y2', [128, 128])
                OT = [sbt(ph, f'OT{i}', [128, 512], BF16) for i in range(2)]
                QSCALE = 128.0 ** -0.5

                def phase1(hl, tt):
                    b = tt % 2
                    hb_ = ht[b]
                    load_h_tile(hb_, ('ht', b), tt * 512)
                    wv = Wh[:].rearrange("p t (k c) -> p (t k) c", c=512)
                    for s_ in range(4):
                        pa = PA[s_ % 2]; pk = ('PA', s_ % 2)
                        for kc in range(KC):
                            Sd.op('pe', lambda E_, pa=pa, kc=kc, s_=s_: E_.matmul(pa[:, :], wv[:, kc, s_ * 128:(s_ + 1) * 128], hb_[:, kc, :],
                                                                         start=(kc == 0), stop=(kc == KC - 1)),
                                  reads=['Wh', ('ht', b)], writes=[pk], inc=(kc == KC - 1))
                            if kc % 4 == 3 and kc != KC - 1:
                                yield
                        if s_ < 3:
                            pc_, po_ = pcb[s_], post[s_]
                            Sd.op('act', lambda E_, pa=pa, pc_=pc_: E_.activation(pc_[:, 3:515], pa[:, :], AF.Copy), reads=[pk], writes=[('pcb', s_)])
                            Sd.op('dve', lambda E_, pc_=pc_, po_=po_, s_=s_: E_.tensor_scalar(po_[:], pc_[:, 3:515], GCV[:, hl, s_, 3:4], None, op0=ALU.mult),
                                  reads=[('pcb', s_), 'GCV'], writes=[('post', s_)])
                            for tap in (2, 1, 0):
                                Sd.op('dve', lambda E_, pc_=pc_, po_=po_, s_=s_, tap=tap: E_.scalar_tensor_tensor(
                                    po_[:], pc_[:, tap:tap + 512], GCV[:, hl, s_, tap:tap + 1], po_[:], op0=ALU.mult, op1=ALU.add),
                                    reads=[('pcb', s_), 'GCV', ('post', s_)], writes=[('post', s_)])
                            Sd.op('dve', lambda E_, pc_=pc_: E_.tensor_copy(pc_[:, 0:3], pc_[:, 512:515]), reads=[('pcb', s_)], writes=[('pcb', s_)])
                            Sd.op('act', lambda E_, po_=po_: E_.activation(po_[:], po_[:], AF.Silu), reads=[('post', s_)], writes=[('post', s_)])
                        if s_ < 2:
                            Sd.op('act', lambda E_, po_=po_: E_.activation(sqb[:], po_[:], AF.Square), reads=[('post', s_)], writes=['sqb'])
                            Sd.op('pe', lambda E_, pa=pa: E_.matmul(pa[:, :], ONB[:], sqb[:], start=True, stop=True), reads=['sqb', 'ONB'], writes=[pk])
                            Sd.op('dve', lambda E_, pa=pa: E_.tensor_scalar(rs[:], pa[:, :], EPS, None, op0=ALU.add), reads=[pk], writes=['rs'])
                            Sd.op('act', lambda E_: E_.activation(rs[:], rs[:], AF.Sqrt), reads=['rs'], writes=['rs'])
                            Sd.op('dve', lambda E_: E_.reciprocal(rs[:], rs[:]), reads=['rs'], writes=['rs'])
                            if s_ == 0:
                                Sd.op('dve', lambda E_, po_=po_: E_.scalar_tensor_tensor(QT[b][:], po_[:], QSCALE, rs[:], op0=ALU.mult, op1=ALU.mult),
                                      reads=[('post', 0), 'rs'], writes=[('QT', b)])
                            else:
                                Sd.op('dve', lambda E_, po_=po_: E_.tensor_tensor(KT[b][:], po_[:], rs[:], ALU.mult), reads=[('post', 1), 'rs'], writes=[('KT', b)])
                                for blk in range(4):
                                    Sd.op('pe', lambda E_, pa=pa, blk=blk: E_.transpose(pa[:, blk * 128:(blk + 1) * 128], KT[b][:, blk * 128:(blk + 1) * 128], IDENT),
                                          reads=[('KT', b), 'CON'], writes=[pk], inc=(blk == 3))
                                Sd.op('act', lambda E_, pa=pa: E_.activation(KTM[b][:].rearrange("p a c -> p (a c)"), pa[:, :], AF.Copy), reads=[pk], writes=[('KTM', b)])
                        if s_ == 2:
                            for blk in range(4):
                                Sd.op('pe', lambda E_, pa=pa, blk=blk, po_=po_: E_.transpose(pa[:, blk * 128:(blk + 1) * 128], po_[:, blk * 128:(blk + 1) * 128], IDENT),
                                      reads=[('post', 2), 'CON'], writes=[pk], inc=(blk == 3))
                            for blk in range(4):
                                Sd.op('act', lambda E_, pa=pa, blk=blk: E_.activation(VB[b][:, blk, :], pa[:, blk * 128:(blk + 1) * 128], AF.Copy,
                                                                              scale=BETA[:, hl, tt * 4 + blk:tt * 4 + blk + 1]),
                                      reads=[pk, 'BETA'], writes=[('VB', b)])
                        if s_ == 3:
                            Sd.op('act', lambda E_, pa=pa: E_.activation(zsf[:], pa[:, :], AF.Silu), reads=[pk], writes=['zsf'])
                            for blk in range(4):
                                Sd.op('pe', lambda E_, pa=pa, blk=blk: E_.transpose(pa[:, blk * 128:(blk + 1) * 128], zsf[:, blk * 128:(blk + 1) * 128], IDENT),
                                      reads=['zsf', 'CON'], writes=[pk], inc=(blk == 3))
                            Sd.op('act', lambda E_, pa=pa: E_.activation(ZS[tt % 3][:].rearrange("p a c -> p (a c)"), pa[:, :], AF.Copy), reads=[pk], writes=[('ZS', tt % 3)])
                        yield

                def phase2(hl, tt, blk):
                    b = tt % 2
                    B_ = BB[blk]; pb = PBk[blk]; pk = ('PB', blk); P_ = P3[b]
                    g = tt * 4 + blk
                    cs = slice(blk * 128, (blk + 1) * 128)
                    la = LA[:, hl, g:g + 1]; beta = BETA[:, hl, g:g + 1]
                    sc = B_['sc']
                    k = lambda nm: (nm, blk)
                    Sd.op('dve', lambda E_: E_.tensor_scalar(B_['LAB'][:], ONES, la, None, op0=ALU.mult), reads=['CON', 'LA'], writes=[k('LAB')])
                    Sd.op('pe', lambda E_: E_.matmul(pb[:, 0:128], B_['LAB'][:], UT, start=True, stop=True), reads=[k('LAB'), 'CON'], writes=[pk], inc=False)
                    Sd.op('pe', lambda E_: E_.matmul(pb[:, 128:129], UT, la, start=True, stop=True), reads=['LA', 'CON'], writes=[pk])
                    Sd.op('act', lambda E_: E_.activation(B_['GROW'][:], pb[:, 0:128], AF.Copy), reads=[pk], writes=[k('GROW')])
                    Sd.op('act', lambda E_: E_.activation(sc[:, 0:1], pb[:, 128:129], AF.Copy), reads=[pk], writes=[k('sc')])
                    yield
                    gcol = sc[:, 0:1]; glast = B_['GROW'][:, 127:128]
                    Sd.op('act', lambda E_: E_.activation(sc[:, 1:2], gcol, AF.Exp), reads=[k('sc')], writes=[k('sc')])
                    Sd.op('act', lambda E_: E_.activation(sc[:, 2:3], gcol, AF.Exp, scale=-1.0, bias=glast), reads=[k('sc'), k('GROW')], writes=[k('sc')])
                    Sd.op('act', lambda E_: E_.activation(P_['EGL'][:, blk:blk + 1], glast, AF.Exp), reads=[k('GROW')], writes=[('EGL', b)])
                    Sd.op('act', lambda E_: E_.activation(B_['EGR'][:], B_['GROW'][:], AF.Exp), reads=[k('GROW')], writes=[k('EGR')])
                    Sd.op('dve', lambda E_: E_.scalar_tensor_tensor(B_['T1'][:], B_['GROW'][:], gcol, POSS, op0=ALU.subtract, op1=ALU.max),
                          reads=[k('GROW'), k('sc'), 'CON'], writes=[k('T1')])
                    Sd.op('act', lambda E_: E_.activation(B_['DM'][:], B_['T1'][:], AF.Exp, scale=-1.0), reads=[k('T1')], writes=[k('DM')])
                    Sd.op('dve', lambda E_: E_.scalar_tensor_tensor(B_['T1'][:], B_['GROW'][:], gcol, NEGT, op0=ALU.subtract, op1=ALU.min),
                          reads=[k('GROW'), k('sc'), 'CON', k('DM')], writes=[k('T1')])
                    Sd.op('act', lambda E_: E_.activation(B_['DTM'][:], B_['T1'][:], AF.Exp), reads=[k('T1')], writes=[k('DTM')])
                    Sd.op('dve', lambda E_: E_.tensor_tensor(sc[:, 4:5], beta, sc[:, 1:2], ALU.mult), reads=['BETA', k('sc')], writes=[k('sc')])
                    yield
                    Sd.op('pe', lambda E_: E_.matmul(pb[:, 0:128], KT[b][:, cs], KT[b][:, cs], start=True, stop=True), reads=[('KT', b)], writes=[pk])
                    Sd.op('dve', lambda E_: E_.scalar_tensor_tensor(B_['M0'][:], pb[:, 0:128], beta, B_['DM'][:], op0=ALU.mult, op1=ALU.mult),
                          reads=[pk, 'BETA', k('DM')], writes=[k('M0')])
                    yield
                    Sd.op('pe', lambda E_: E_.transpose(pb[:, 0:128], B_['M0'][:], IDENT), reads=[k('M0'), 'CON'], writes=[pk])
                    Sd.op('act', lambda E_: E_.activation(B_['N0'][:], pb[:, 0:128], AF.Copy), reads=[pk], writes=[k('N0')])
                    Sd.op('dve', lambda E_: E_.tensor_tensor(B_['R0'][:], IDENT, B_['N0'][:], ALU.subtract), reads=['CON', k('N0')], writes=[k('R0')])
                    yield
                    mi = 0
                    for lev in range(1, 8):
                        if (1 << lev) >= 128 * 2:
                            break
                        Mc, Nc_, Mn, Nn = B_[f'M{mi}'], B_[f'N{mi}'], B_[f'M{1 - mi}'], B_[f'N{1 - mi}']
                        Rc, Rn = B_[f'R{mi}'], B_[f'R{1 - mi}']
                        kM, kN, kMn, kNn, kR, kRn = k(f'M{mi}'), k(f'N{mi}'), k(f'M{1 - mi}'), k(f'N{1 - mi}'), k(f'R{mi}'), k(f'R{1 - mi}')
                        lastlev = (1 << (lev + 1)) >= 256
                        Sd.op('pe', lambda E_, Mc=Mc, Nc_=Nc_: E_.matmul(pb[:, 0:128], Nc_[:], Mc[:], start=True, stop=True), reads=[kM, kN], writes=[pk])
                        Sd.op('act', lambda E_, Mn=Mn: E_.activation(Mn[:], pb[:, 0:128], AF.Copy), reads=[pk], writes=[kMn])
                        Sd.op('dve', lambda E_: E_.tensor_tensor(B_['IM'][:], pb[:, 0:128], IDENT, ALU.add), reads=[pk, 'CON'], writes=[k('IM')])
                        yield
                        if not lastlev:
                            Sd.op('pe', lambda E_, Mc=Mc, Nc_=Nc_: E_.matmul(pb[:, 0:128], Mc[:], Nc_[:], start=True, stop=True), reads=[kM, kN], writes=[pk])
                            Sd.op('act', lambda E_, Nn=Nn: E_.activation(Nn[:], pb[:, 0:128], AF.Copy), reads=[pk], writes=[kNn])
                            yield
                        Sd.op('pe', lambda E_, Rc=Rc: E_.matmul(pb[:, 0:128], B_['IM'][:], Rc[:], start=True, stop=True), reads=[k('IM'), kR], writes=[pk])
                        Sd.op('dve', lambda E_, Rn=Rn: E_.tensor_copy(Rn[:], pb[:, 0:128]), reads=[pk], writes=[kRn])
                        yield
                        mi = 1 - mi
                    RT = B_[f'R{mi}']; kRT = k(f'R{mi}')
                    Sd.op('pe', lambda E_: E_.matmul(pb[:, 0:128], RT[:], VB[b][:, blk, :], start=True, stop=True), reads=[kRT, ('VB', b)], writes=[pk])
                    Sd.op('dve', lambda E_: E_.tensor_copy(P_['U'][:, blk, :], pb[:, 0:128]), reads=[pk], writes=[('U', b)])
                    Sd.op('dve', lambda E_: E_.tensor_scalar(B_['KBG'][:], KTM[b][:, blk, :], sc[:, 4:5], None, op0=ALU.mult), reads=[('KTM', b), k('sc')], writes=[k('KBG')])
                    yield
                    Sd.op('pe', lambda E_: E_.matmul(pb[:, 0:128], B_['KBG'][:], RT[:], start=True, stop=True), reads=[kRT, k('KBG')], writes=[pk])
                    Sd.op('dve', lambda E_: E_.tensor_copy(P_['WT'][:, blk, :], pb[:, 0:128]), reads=[pk], writes=[('WT', b)])
                    yield
                    Sd.op('pe', lambda E_: E_.matmul(pb[:, 0:128], KT[b][:, cs], QT[b][:, cs], start=True, stop=True), reads=[('KT', b), ('QT', b)], writes=[pk])
                    Sd.op('dve', lambda E_: E_.tensor_tensor(P_['ATT'][:, blk, :], pb[:, 0:128], B_['DTM'][:], ALU.mult), reads=[pk, k('DTM')], writes=[('ATT', b)])
                    Sd.op('dve', lambda E_: E_.tensor_tensor(P_['QHT'][:, blk, :], QT[b][:, cs], B_['EGR'][:], ALU.mult), reads=[('QT', b), k('EGR')], writes=[('QHT', b)])
                    Sd.op('dve', lambda E_: E_.tensor_scalar(P_['KTL'][:, blk, :], KTM[b][:, blk, :], sc[:, 2:3], None, op0=ALU.mult), reads=[('KTM', b), k('sc')], writes=[('KTL', b)])
                    yield

                def phase3(hl, tt):
                    b = tt % 2
                    P_ = P3[b]
                    for blk in range(4):
                        p0, p1 = PC[0], PC[1]
                        Sd.op('pe', lambda E_, blk=blk: E_.matmul(p0[:, 0:128], P_['WT'][:, blk, :], Sst[:], start=True, stop=True), reads=[('WT', b), 'Sst'], writes=[('PC', 0)])
                        Sd.op('dve', lambda E_, blk=blk: E_.tensor_tensor(VN[:], P_['U'][:, blk, :], p0[:, 0:128], ALU.subtract), reads=[('U', b), ('PC', 0)], writes=['VN'])
                        yield
                        Sd.op('pe', lambda E_, blk=blk: E_.matmul(p1[:, 0:128], P_['QHT'][:, blk, :], Sst[:], start=True, stop=False), reads=[('QHT', b), 'Sst'], writes=[('PC', 1)], inc=False)
                        Sd.op('pe', lambda E_, blk=blk: E_.matmul(p1[:, 0:128], P_['ATT'][:, blk, :], VN[:], start=False, stop=True), reads=[('ATT', b), 'VN'], writes=[('PC', 1)])
                        Sd.op('pe', lambda E_, blk=blk: E_.matmul(p0[:, 0:128], P_['KTL'][:, blk, :], VN[:], start=True, stop=True), reads=[('KTL', b), 'VN'], writes=[('PC', 0)])
                        Sd.op('dve', lambda E_, blk=blk: E_.scalar_tensor_tensor(Sst[:], Sst[:], P_['EGL'][:, blk:blk + 1], p0[:, 0:128], op0=ALU.mult, op1=ALU.add),
                              reads=['Sst', ('EGL', b), ('PC', 0)], writes=['Sst'])
                        Sd.op('dve', lambda E_, blk=blk: E_.memset(ss[:, blk:blk + 1], 0.0), writes=['ss'])
                        Sd.op('act', lambda E_, blk=blk: E_.activation(junk[:], p1[:, 0:128], AF.Square, accum_out=ss[:, blk:blk + 1]), reads=[('PC', 1), 'ss'], writes=['ss', 'junk'])
                        Sd.op('dve', lambda E_, blk=blk: E_.tensor_scalar(ss[:, blk:blk + 1], ss[:, blk:blk + 1], 1.0 / 128, EPS, op0=ALU.mult, op1=ALU.add), reads=['ss'], writes=['ss'])
                        Sd.op('act', lambda E_, blk=blk: E_.activation(ss[:, blk:blk + 1], ss[:, blk:blk + 1], AF.Sqrt), reads=['ss'], writes=['ss'])
                        Sd.op('dve', lambda E_, blk=blk: E_.reciprocal(ss[:, blk:blk + 1], ss[:, blk:blk + 1]), reads=['ss'], writes=['ss'])
                        Sd.op('dve', lambda E_, blk=blk: E_.scalar_tensor_tensor(Y1[:], p1[:, 0:128], ss[:, blk:blk + 1], GN[:], op0=ALU.mult, op1=ALU.mult),
                              reads=[('PC', 1), 'ss', 'GN'], writes=['Y1'])
                        Sd.op('dve', lambda E_, blk=blk: E_.tensor_tensor(Y2[:], Y1[:], ZS[tt % 3][:, blk, :], ALU.mult), reads=['Y1', ('ZS', tt % 3)], writes=['Y2'])
                        yield
                        Sd.op('pe', lambda E_, blk=blk: E_.transpose(p1[:, 0:128], Y2[:], IDENT), reads=['Y2', 'CON'], writes=[('PC', 1)])
                        Sd.op('act', lambda E_, blk=blk: E_.activation(OT[b][:, blk * 128:(blk + 1) * 128], p1[:, 0:128], AF.Copy), reads=[('PC', 1)], writes=[('OT', b)])
                        yield
                    Sd.dma('act', oloc[hl][:, tt * 512:(tt + 1) * 512], OT[b][:], reads=[('OT', b)], writes=[('oloc', hl)], stream=f'st_o{b}')

                rounds_per_head = -(-RPL // HL)
                GSTOP = getattr(cfg, 'gdn_stop', 9)
                for hl in range(HL if GSTOP > 1 else 0):
                    src = WIN[l][hl][:, :].rearrange("(t p) e -> p t e", p=128)
                    Sd.dma('sp', Wh[:], src, reads=[('WIN', l, hl)], writes=['Wh'])
                    Sd.op('dve', lambda E_: E_.memset(Sst[:], 0.0), writes=['Sst'])
                    for s_ in range(3):
                        Sd.op('dve', lambda E_, s_=s_: E_.memset(pcb[s_][:, 0:3], 0.0), writes=[('pcb', s_)])
                    for st_ in range(NTT + 2):
                        gens = []
                        if st_ < NTT:
                            gens.append(phase1(hl, st_))
                        if 1 <= st_ <= NTT and GSTOP > 2:
                            gens += [phase2(hl, st_ - 1, blk) for blk in range(4)]
                        if st_ >= 2 and GSTOP > 3:
                            gens.append(phase3(hl, st_ - 2))
                        interleave(gens)
                    drain(rounds_per_head)
                    if GSTOP > 3:
                        Sd.cc('AllGather', G4, oloc[hl][:, :], OG[hl][:, :], reads=[('oloc', hl)], writes=[('OG', hl)])
                if 'oloc0' in dbg and GSTOP > 3:
                    Sd.dma('sp', dbg['oloc0'][:, :], oloc[0][:, :], reads=[('oloc', 0)], stream='dbg3')
                Sd.barrier()
                Sd.emit()


        def phase_fox(l):
            NTT = S // 512
            with ExitStack() as ph:
                Wh = sbt(ph, 'Wh', [128, 4, E], BF16)
                Wg = sbt(ph, 'Wg', [128, KC, 2 * HL], BF16)
                Wgf = sbt(ph, 'Wgf', [128, KC, 2 * HL])
                ht = [sbt(ph, f'ht{i}', [128, KC, 512], BF16) for i in range(2)]
                GT = sbt(ph, 'GT', [128, NB, HL])
                LF = sbt(ph, 'LF', [128, HL, NB])
                HR = sbt(ph, 'HR', [128, 3 * HL]); FQK = sbt(ph, 'FQK', [128, 2]); FQS = sbt(ph, 'FQS', [128, 2])
                ONB = sbt(ph, 'ONB', [128, 128], BF16); SU = sbt(ph, 'SU', [128, 128])
                gt = [sbt(ph, f'gtmp{i}', [128, NB]) for i in range(4)]
                PA = [pst(ph, f'PA{i}', [128, 512]) for i in range(2)]
                PS = [pst(ph, f'PS{i}', [128, 512]) for i in range(2)]
                PO = [pst(ph, f'PO{i}', [128, 512]) for i in range(2)]
                PL = [pst(ph, f'PL{i}', [128, 512]) for i in range(2)]
                Sd.dma('sp', HR[:], hrow[:, :], writes=['HR'])
                Sd.dma('sp', FQK[:], fqk[:, :], writes=['FQK'])
                Sd.dma('sp', Wgf[:].rearrange("p a b -> p (a b)"), wg[:, l * KC * 2 * HL:(l + 1) * KC * 2 * HL], writes=['Wgf'])
                Sd.op('dve', lambda E_: E_.tensor_copy(Wg[:], Wgf[:]), reads=['Wgf'], writes=['Wg'])
                Sd.op('dve', lambda E_: E_.tensor_copy(ONB[:], ONES), reads=['CON'], writes=['ONB'])
                Sd.op('dve', lambda E_: E_.tensor_tensor(SU[:], UT, IDENT, ALU.subtract), reads=['CON'], writes=['SU'])
                Sd.op('dve', lambda E_: E_.tensor_scalar(FQS[:, 0:1], FQK[:, 0:1], 128.0 ** -0.5, None, op0=ALU.mult), reads=['FQK'], writes=['FQS'])
                Sd.op('dve', lambda E_: E_.tensor_copy(FQS[:, 1:2], FQK[:, 1:2]), reads=['FQK', 'FQS'], writes=['FQS'])
                for tt in range(NTT):
                    hb_ = ht[tt % 2]
                    load_h_tile(hb_, ('ht', tt % 2), tt * 512)
                    for blk in range(4):
                        for kc in range(KC):
                            Sd.op('pe', lambda E_, hb_=hb_, kc=kc, blk=blk: E_.matmul(
                                PA[blk % 2][:, 0:HL], hb_[:, kc, blk * 128:(blk + 1) * 128], Wg[:, kc, 0:HL],
                                start=(kc == 0), stop=(kc == KC - 1)), reads=[('ht', tt % 2), 'Wg'], writes=[('PA', blk % 2)], inc=(kc == KC - 1))
                        Sd.op('act', lambda E_, blk=blk, tt=tt: E_.activation(GT[:, tt * 4 + blk, :], PA[blk % 2][:, 0:HL], AF.Copy),
                              reads=[('PA', blk % 2)], writes=['GT'])
                for hl in range(HL):
                    fr = GT[:, :, hl]
                    Sd.op('dve', lambda E_, hl=hl, fr=fr: E_.tensor_scalar(gt[0][:], fr, HR[:, 2 * HL + hl:2 * HL + hl + 1], None, op0=ALU.add),
                          reads=['GT', 'HR'], writes=['gt0'])
                    Sd.op('act', lambda E_: E_.activation(gt[1][:], gt[0][:], AF.Abs), reads=['gt0'], writes=['gt1'])
                    Sd.op('act', lambda E_: E_.activation(gt[1][:], gt[1][:], AF.Exp, scale=-1.0), reads=['gt1'], writes=['gt1'])
                    Sd.op('act', lambda E_: E_.activation(gt[1][:], gt[1][:], AF.Ln, bias=1.0), reads=['gt1'], writes=['gt1'])
                    Sd.op('dve', lambda E_: E_.tensor_scalar(gt[2][:], gt[0][:], -1.0, 0.0, op0=ALU.mult, op1=ALU.max), reads=['gt0'], writes=['gt2'])
                    Sd.op('dve', lambda E_: E_.tensor_tensor(gt[2][:], gt[2][:], gt[1][:], ALU.add), reads=['gt2', 'gt1'], writes=['gt2'])
                    Sd.op('dve', lambda E_, hl=hl: E_.tensor_scalar(LF[:, hl, :], gt[2][:], -1.0, None, op0=ALU.mult), reads=['gt2'], writes=['LF'])
                KTb = sbt(ph, 'KTb', [128, S], BF16)
                VTM = sbt(ph, 'VTM', [128, NB, 128], BF16)
                CROW = sbt(ph, 'CROW', [128, S])
                CK = sbt(ph, 'CK', [128, NB]); CKN = sbt(ph, 'CKN', [128, NB]); OFF = sbt(ph, 'OFF', [128, NB]); TOTT = sbt(ph, 'TOTT', [128, 128])
                DG = [sbt(ph, f'DG{i}', [128, 128]) for i in range(2)]
                QTb = [sbt(ph, f'QTb{i}', [128, 512], BF16) for i in range(2)]
                GS = [sbt(ph, f'GS{i}', [128, 512], BF16) for i in range(2)]
                qf = sbt(ph, 'qf', [128, 512]); sqb = sbt(ph, 'sqb', [128, 512], BF16); rs = sbt(ph, 'rs', [128, 512])
                Lb = [sbt(ph, f'Lb{i}', [128, 512]) for i in range(2)]
                PT = [sbt(ph, f'PT{i}', [128, 512], BF16) for i in range(2)]
                rl = sbt(ph, 'rl', [128, 512]); otmp = sbt(ph, 'otmp', [128, 512])
                OT = [sbt(ph, f'OT{i}', [128, 512], BF16) for i in range(2)]

                def cum_head(hl):
                    lf = LF[:, hl, :]
                    Sd.op('pe', lambda E_: E_.matmul(PA[0][:, 0:NB], UT, lf, start=True, stop=True), reads=['LF', 'CON'], writes=[('PA', 0)])
                    Sd.op('pe', lambda E_: E_.matmul(PA[1][0:NB, 0:128], lf, ONES, start=True, stop=True), reads=['LF', 'CON'], writes=[('PA', 1)])
                    Sd.op('act', lambda E_: E_.activation(TOTT[0:NB, :], PA[1][0:NB, 0:128], AF.Copy), reads=[('PA', 1)], writes=['TOTT'])
                    Sd.op('act', lambda E_: E_.activation(CK[:], PA[0][:, 0:NB], AF.Copy), reads=[('PA', 0)], writes=['CK'])
                    Sd.op('pe', lambda E_: E_.matmul(PA[1][:, 0:NB], TOTT[0:NB, :], SU[0:NB, 0:NB], start=True, stop=True), reads=['TOTT', 'SU'], writes=[('PA', 1)])
                    Sd.op('dve', lambda E_: E_.tensor_tensor(CK[:], CK[:], PA[1][:, 0:NB], ALU.add), reads=['CK', ('PA', 1)], writes=['CK'])
                    Sd.op('dve', lambda E_: E_.tensor_scalar(CKN[:], CK[:], -1.0, None, op0=ALU.mult), reads=['CK'], writes=['CKN'])
                    for g4 in range(NB // 4):
                        pa = PA[g4 % 2]; pk = ('PA', g4 % 2)
                        for i4 in range(4):
                            bb = g4 * 4 + i4
                            dg = DG[bb % 2]
                            Sd.op('dve', lambda E_, dg=dg, bb=bb: E_.tensor_scalar(dg[:], IDENT, CK[:, bb:bb + 1], None, op0=ALU.mult),
                                  reads=['CON', 'CK'], writes=[('DG', bb % 2)])
                            Sd.op('pe', lambda E_, dg=dg, pa=pa, i4=i4: E_.matmul(pa[:, i4 * 128:(i4 + 1) * 128], ONES, dg[:], start=True, stop=True),
                                  reads=[('DG', bb % 2), 'CON'], writes=[pk])
                        Sd.op('act', lambda E_, pa=pa, g4=g4: E_.activation(CROW[:, g4 * 512:(g4 + 1) * 512], pa[:, :], AF.Copy), reads=[pk], writes=['CROW'])
                    yield

                def f1(hl, tt):
                    b = tt % 2
                    hb_ = ht[b]
                    load_h_tile(hb_, ('ht', b), tt * 512)
                    wv = Wh[:].rearrange("p t (k c) -> p (t k) c", c=512)
                    for s_ in range(4):
                        pa = PA[s_ % 2]; pk = ('PA', s_ % 2)
                        for kc in range(KC):
                            Sd.op('pe', lambda E_, pa=pa, kc=kc, s_=s_: E_.matmul(pa[:, :], wv[:, kc, s_ * 128:(s_ + 1) * 128], hb_[:, kc, :],
                                                                         start=(kc == 0), stop=(kc == KC - 1)),
                                  reads=['Wh', ('ht', b)], writes=[pk], inc=(kc == KC - 1))
                            if kc % 4 == 3 and kc != KC - 1:
                                yield
                        if s_ < 2:
                            Sd.op('act', lambda E_, pa=pa: E_.activation(qf[:], pa[:, :], AF.Copy), reads=[pk], writes=['qf'])
                            Sd.op('act', lambda E_: E_.activation(sqb[:], qf[:], AF.Square), reads=['qf'], writes=['sqb'])
                            Sd.op('pe', lambda E_, pa=pa: E_.matmul(pa[:, :], ONB[:], sqb[:], start=True, stop=True), reads=['sqb', 'ONB'], writes=[pk])
                            Sd.op('dve', lambda E_, pa=pa: E_.tensor_scalar(rs[:], pa[:, :], 1.0 / 128, EPS, op0=ALU.mult, op1=ALU.add), reads=[pk], writes=['rs'])
                            Sd.op('act', lambda E_: E_.activation(rs[:], rs[:], AF.Sqrt), reads=['rs'], writes=['rs'])
                            Sd.op('dve', lambda E_: E_.reciprocal(rs[:], rs[:]), reads=['rs'], writes=['rs'])
                            if s_ == 0:
                                Sd.op('dve', lambda E_: E_.scalar_tensor_tensor(QTb[b][:], qf[:], FQS[:, 0:1], rs[:], op0=ALU.mult, op1=ALU.mult),
                                      reads=['qf', 'rs', 'FQS'], writes=[('QTb', b)])
                            else:
                                Sd.op('dve', lambda E_: E_.scalar_tensor_tensor(KTb[:, tt * 512:(tt + 1) * 512], qf[:], FQS[:, 1:2], rs[:], op0=ALU.mult, op1=ALU.mult),
                                      reads=['qf', 'rs', 'FQS'], writes=[('KTb', tt)])
                        if s_ == 2:
                            Sd.op('act', lambda E_, pa=pa: E_.activation(qf[:], pa[:, :], AF.Copy), reads=[pk], writes=['qf'])
                            for blk in range(4):
                                Sd.op('pe', lambda E_, pa=pa, blk=blk: E_.transpose(pa[:, blk * 128:(blk + 1) * 128], qf[:, blk * 128:(blk + 1) * 128], IDENT),
                                      reads=['qf', 'CON'], writes=[pk], inc=(blk == 3))
                            Sd.op('act', lambda E_, pa=pa: E_.activation(VTM[:, tt * 4:tt * 4 + 4, :].rearrange("p a c -> p (a c)"), pa[:, :], AF.Copy),
                                  reads=[pk], writes=[('VTM', tt)])
                        if s_ == 3:
                            Sd.op('act', lambda E_, pa=pa: E_.activation(GS[b][:], pa[:, :], AF.Sigmoid), reads=[pk], writes=[('GS', b)])
                        yield

                cnt = [0]

                def f2(hl, qt):
                    b = qt % 2
                    po, pl = PO[b], PL[b]
                    nkb = 4 * qt + 4

                    def qk(kb):
                        c_lo = max(0, kb - 4 * qt) * 128
                        n = 512 - c_lo
                        i2 = cnt[0] % 2
                        cnt[0] += 1
                        ps, Lt, Pt = PS[i2], Lb[i2], PT[i2]
                        qcols = slice(qt * 512 + c_lo, (qt + 1) * 512)
                        Sd.op('pe', lambda E_: E_.matmul(ps[:, 0:n], KTb[:, kb * 128:(kb + 1) * 128], QTb[b][:, c_lo:512], start=True, stop=True),
                              reads=[('KTb', kb // 4), ('QTb', b)], writes=[('PS', i2)])
                        Sd.op('dve', lambda E_: E_.tensor_tensor(Lt[:, 0:n], ps[:, 0:n], CROW[:, qcols], ALU.add),
                              reads=[('PS', i2), 'CROW'], writes=[('Lb', i2)])
                        if kb >= 4 * qt:
                            Sd.op('dve', lambda E_: E_.tensor_tensor(Lt[:, 0:128], Lt[:, 0:128], NEGT, ALU.add), reads=[('Lb', i2), 'CON'], writes=[('Lb', i2)])
                        Sd.op('act', lambda E_: E_.activation(Pt[:, 0:n], Lt[:, 0:n], AF.Exp, bias=CKN[:, kb:kb + 1]),
                              reads=[('Lb', i2), 'CKN'], writes=[('PT', i2)])
                        return (kb, c_lo, n, i2, Pt)

                    def pv(kb, c_lo, n, i2, Pt):
                        Sd.op('pe', lambda E_: E_.matmul(po[:, c_lo:512], VTM[:, kb, :], Pt[:, 0:n], start=(kb == 0), stop=(kb == nkb - 1)),
                              reads=[('VTM', kb // 4), ('PT', i2)], writes=[('PO', b)], inc=False)
                        Sd.op('pe', lambda E_: E_.matmul(pl[:, c_lo:512], ONB[:], Pt[:, 0:n], start=(kb == 0), stop=(kb == nkb - 1)),
                              reads=['ONB', ('PT', i2)], writes=[('PL', b)])

                    pendq = None
                    for kb in range(nkb):
                        st = qk(kb)
                        if pendq is not None:
                            pv(*pendq)
                        pendq = st
                        yield
                    pv(*pendq)
                    Sd.op('dve', lambda E_: E_.reciprocal(rl[:], pl[:, :]), reads=[('PL', b)], writes=['rl'])
                    Sd.op('dve', lambda E_: E_.tensor_tensor(otmp[:], po[:, :], rl[:], ALU.mult), reads=[('PO', b), 'rl'], writes=['otmp'])
                    Sd.op('dve', lambda E_: E_.tensor_tensor(OT[b][:], otmp[:], GS[b][:], ALU.mult), reads=['otmp', ('GS', b)], writes=[('OT', b)])
                    Sd.dma('act', oloc[hl][:, qt * 512:(qt + 1) * 512], OT[b][:], reads=[('OT', b)], writes=[('oloc', hl)], stream=f'st_o{b}')

                rounds_per_head = -(-RPL // HL)
                for hl in range(HL):
                    src = WIN[l][hl][:, :].rearrange("(t p) e -> p t e", p=128)
                    Sd.dma('sp', Wh[:], src, reads=[('WIN', l, hl)], writes=['Wh'])
                    interleave([cum_head(hl)])
                    prev = None
                    for tt in range(NTT):
                        interleave([f1(hl, tt)] + ([f2(hl, prev)] if prev is not None else []))
                        prev = tt
                    interleave([f2(hl, prev)])
                    drain(rounds_per_head)
                    Sd.cc('AllGather', G4, oloc[hl][:, :], OG[hl][:, :], reads=[('oloc', hl)], writes=[('OG', hl)])
                if 'oloc1' in dbg:
                    Sd.dma('sp', dbg['oloc1'][:, :], oloc[0][:, :], reads=[('oloc', 0)], stream='dbg4')
                Sd.barrier()
                Sd.emit()

        RPL = TPLP // 16
        if getattr(cfg, 'stop_after', '') == 'modA':
            Sd.barrier(); Sd.emit()
            return nc
        phase_n0()
        for l in range(L):
            if getattr(cfg, 'fake_o', False) and l in getattr(cfg, 'fake_layers', (0, 1)):
                ofake = din(f'ofake{l}', [HL * 512, S], BF16)
                for hl in range(HL):
                    Sd.dma('sp', OG[hl][:, :], ofake[hl * 512:(hl + 1) * 512, :], writes=[('OG', hl)], stream=f'fk{hl}')
                drain_to((l + 1) * RPL)
            else:
                if l + 1 < L:
                    pass
                mixer = phase_gdn if l == 0 else phase_fox
                mixer(l)
                if getattr(cfg, 'stop_after', '') == ('gdn', 'fox')[l]:
                    break
                drain_to((l + 1) * RPL)
            if l + 1 < L:
                cast_win(l + 1)
                gather_win(l + 1)
            phase_tok(l)
        Sd.barrier()
        Sd.emit()
    return nc


def kernel(**inputs):
    cfg = Cfg(4096, 4096)
    maps = prep_inputs(cfg, inputs)
    nc = build(cfg)
    res = run_bass_kernel_spmd(nc, maps, core_ids=list(range(8)))
    out = np.empty((cfg.B, cfg.S, cfg.D), np.float32)
    for c in range(8):
        d, r = c // 4, c % 4
        out[d, r * cfg.TL:(r + 1) * cfg.TL, :] = np.asarray(res.results[c]['outT']).T
    return out
```
